# Optimizing a Trainium2 kernel written in Bass

```python
import math
import jax, jax.numpy as jnp
from jax import lax
import numpy as np

D_MODEL = 2048
BATCH = 16
SEQ = 256
DEPTH = 4
DEC_BATCH = 4
DEC_SEQ = 4096
PAST_LEN = 512

GRID_W = 64
F_GROUPS = 4
F_GROUP_DIM = 256
F_DIM = F_GROUPS * F_GROUP_DIM
N_HEADS = 8
QK_DIM = 64
V_DIM = 2 * QK_DIM
QK_W = N_HEADS * 2 * QK_DIM
V_W = N_HEADS * V_DIM
SPLIT_SIZES = (F_DIM, F_DIM, QK_W, QK_W, V_W, V_W, D_MODEL, D_MODEL)
IN_COLS = sum(SPLIT_SIZES)
Q_BLOCK = 128
ROPE_BASE = 10000.0
EPS = 1e-6

kernel_name = "hybrid_fourier_diffattn_dit_step"


def rms_norm(x, g):
    xf = x.astype(jnp.float32)
    y = xf * lax.rsqrt(jnp.mean(xf * xf, axis=-1, keepdims=True) + EPS)
    return (y * g.astype(jnp.float32)).astype(x.dtype)


def axial_rope_tables(n_tokens):
    rows = n_tokens // GRID_W
    r = jnp.repeat(jnp.arange(rows, dtype=jnp.float32), GRID_W)
    cidx = jnp.tile(jnp.arange(GRID_W, dtype=jnp.float32), rows)
    half = QK_DIM // 2
    inv = 1.0 / (ROPE_BASE ** (jnp.arange(half // 2, dtype=jnp.float32) * 2.0 / half))
    ang_r = r[:, None] * inv[None, :]
    ang_c = cidx[:, None] * inv[None, :]
    return (jnp.cos(ang_r), jnp.sin(ang_r), jnp.cos(ang_c), jnp.sin(ang_c))


def _rot_half(x, cos, sin):
    x1, x2 = jnp.split(x, 2, axis=-1)
    cos = cos[None, :, None, None, :]
    sin = sin[None, :, None, None, :]
    return jnp.concatenate([x1 * cos - x2 * sin, x2 * cos + x1 * sin], axis=-1)


def apply_axial_rope(x, rope):
    cos_r, sin_r, cos_c, sin_c = rope
    xf = x.astype(jnp.float32)
    xr, xc = jnp.split(xf, 2, axis=-1)
    out = jnp.concatenate([_rot_half(xr, cos_r, sin_r), _rot_half(xc, cos_c, sin_c)], axis=-1)
    return out.astype(x.dtype)


def diff_attention(q, k, v, lam, g_sub, lam_init):
    b, nq = q.shape[0], q.shape[1]
    nblk = nq // Q_BLOCK
    scale = QK_DIM ** -0.5
    kf = k.astype(jnp.float32)
    vf = v.astype(jnp.float32)
    qb = q.reshape(b, nblk, Q_BLOCK, N_HEADS, 2, QK_DIM).transpose(1, 0, 2, 3, 4, 5)

    def one_block(qblk):
        s = jnp.einsum('bqhce,bkhce->bhcqk', qblk.astype(jnp.float32), kf) * scale
        p = jax.nn.softmax(s, axis=-1)
        a = p[:, :, 0] - lam * p[:, :, 1]
        return jnp.einsum('bhqk,bkhd->bqhd', a, vf)

    o = lax.map(one_block, qb)
    o = o.transpose(1, 0, 2, 3, 4).reshape(b, nq, N_HEADS, V_DIM)
    o = o * lax.rsqrt(jnp.mean(o * o, axis=-1, keepdims=True) + EPS)
    o = o * g_sub.astype(jnp.float32) * (1.0 - lam_init)
    return o.astype(q.dtype)


def layer(x, cvec, params, lam_init, rope, k_ctx, v_ctx):
    (w_in, w_fproj, w_aproj, w_out, w_mod, b_mod, g_norm, g_q, g_k, g_sub,
     lq1, lk1, lq2, lk2) = params
    b, n = x.shape[0], x.shape[1]
    mod = jax.nn.silu(cvec) @ w_mod + b_mod
    shift, scale, gate = jnp.split(mod, 3, axis=-1)
    h = rms_norm(x, g_norm) * (1.0 + scale) + shift
    proj = h @ w_in
    idx = [int(i) for i in np.cumsum(SPLIT_SIZES)[:-1]]
    uf, zf, q, k, v, za, gf, ga = jnp.split(proj, idx, axis=-1)

    uf = uf.reshape(b, n, F_GROUPS, F_GROUP_DIM).astype(jnp.float32)
    yf = jnp.fft.fft2(uf, axes=(1, 3), norm='ortho').real.reshape(b, n, F_DIM).astype(x.dtype)
    yf = yf * jax.nn.silu(zf)

    q = rms_norm(q.reshape(b, n, N_HEADS, 2, QK_DIM), g_q)
    k = rms_norm(k.reshape(b, n, N_HEADS, 2, QK_DIM), g_k)
    v = v.reshape(b, n, N_HEADS, V_DIM)
    if rope is not None:
        q = apply_axial_rope(q, rope)
        k = apply_axial_rope(k, rope)
    if k_ctx is None:
        k_all, v_all = k, v
    else:
        k_all = jnp.concatenate([k_ctx.astype(k.dtype), k], axis=1)
        v_all = jnp.concatenate([v_ctx.astype(v.dtype), v], axis=1)
    lam = (jnp.exp(jnp.sum(lq1.astype(jnp.float32) * lk1.astype(jnp.float32)))
           - jnp.exp(jnp.sum(lq2.astype(jnp.float32) * lk2.astype(jnp.float32))) + lam_init)
    ya = diff_attention(q, k_all, v_all, lam, g_sub, lam_init).reshape(b, n, V_W)
    ya = ya * jax.nn.silu(za)

    merged = jax.nn.sigmoid(gf) * (yf @ w_fproj) + jax.nn.sigmoid(ga) * (ya @ w_aproj)
    out = merged @ w_out
    return x + gate * out, k, v


def setup_inputs(seed: int = 0) -> dict:
    key = jax.random.key(seed)
    ks = jax.random.split(key, 20)
    nrm = jax.random.normal
    d = D_MODEL
    return {
        "x_prompt": nrm(ks[0], (BATCH, SEQ, d), jnp.float32),
        "x_sample": nrm(ks[1], (DEC_BATCH, DEC_SEQ, d), jnp.float32),
        "c": nrm(ks[2], (DEC_BATCH, d), jnp.float32),
        "cache_k": nrm(ks[3], (DEC_BATCH, DEPTH, PAST_LEN, N_HEADS, 2, QK_DIM), jnp.float32),
        "cache_v": nrm(ks[4], (DEC_BATCH, DEPTH, PAST_LEN, N_HEADS, V_DIM), jnp.float32),
        "c_ctx": nrm(ks[5], (d,), jnp.float32),
        "w_in": nrm(ks[6], (DEPTH, d, IN_COLS), jnp.float32) * d ** -0.5,
        "w_fproj": nrm(ks[7], (DEPTH, F_DIM, d), jnp.float32) * F_DIM ** -0.5,
        "w_aproj": nrm(ks[8], (DEPTH, V_W, d), jnp.float32) * V_W ** -0.5,
        "w_out": nrm(ks[9], (DEPTH, d, d), jnp.float32) * d ** -0.5,
        "w_mod": nrm(ks[10], (DEPTH, d, 3 * d), jnp.float32) * d ** -0.5,
        "b_mod": nrm(ks[11], (DEPTH, 3 * d), jnp.float32) * 0.01,
        "g_norm": 1.0 + 0.05 * nrm(ks[12], (DEPTH, d), jnp.float32),
        "g_q": 1.0 + 0.05 * nrm(ks[13], (DEPTH, QK_DIM), jnp.float32),
        "g_k": 1.0 + 0.05 * nrm(ks[14], (DEPTH, QK_DIM), jnp.float32),
        "g_sub": 1.0 + 0.05 * nrm(ks[15], (DEPTH, V_DIM), jnp.float32),
        "lam_q1": 0.1 * nrm(ks[16], (DEPTH, QK_DIM), jnp.float32),
        "lam_k1": 0.1 * nrm(ks[17], (DEPTH, QK_DIM), jnp.float32),
        "lam_q2": 0.1 * nrm(ks[18], (DEPTH, QK_DIM), jnp.float32),
        "lam_k2": 0.1 * nrm(ks[19], (DEPTH, QK_DIM), jnp.float32),
    }


def reference(x_prompt, x_sample, c, cache_k, cache_v, c_ctx, w_in, w_fproj, w_aproj, w_out,
              w_mod, b_mod, g_norm, g_q, g_k, g_sub, lam_q1, lam_k1, lam_q2, lam_k2):
    rope = axial_rope_tables(x_sample.shape[1])
    cvec_ctx = c_ctx[None, None, :]
    cvec_lat = c[:, None, :]
    y_p = x_prompt
    y_s = x_sample
    ks_out = []
    vs_out = []
    for l in range(DEPTH):
        params = (w_in[l], w_fproj[l], w_aproj[l], w_out[l], w_mod[l], b_mod[l], g_norm[l],
                  g_q[l], g_k[l], g_sub[l], lam_q1[l], lam_k1[l], lam_q2[l], lam_k2[l])
        lam_init = 0.8 - 0.6 * math.exp(-0.3 * l)
        y_p, k_l, v_l = layer(y_p, cvec_ctx, params, lam_init, None, None, None)
        ks_out.append(k_l)
        vs_out.append(v_l)
        y_s, _, _ = layer(y_s, cvec_lat, params, lam_init, rope, cache_k[:, l], cache_v[:, l])
    new_cache_k = jnp.stack(ks_out, axis=1)
    new_cache_v = jnp.stack(vs_out, axis=1)
    return (y_p, y_s, new_cache_k, new_cache_v)
```

```python
import math
import numpy as np
import ml_dtypes
import concourse.bass as bass
import concourse.mybir as mybir
from concourse.bass_utils import run_bass_kernel_spmd

F32 = mybir.dt.float32
BF16 = mybir.dt.bfloat16
AF = mybir.ActivationFunctionType
ALU = mybir.AluOpType
AX = mybir.AxisListType

DEPTH = 4
D = 2048
KC = 16
NH = 8
EPS = 1e-6
NTOK = 2560
NTB = 5
PAIRS = [[0, 1], [2, 3], [4, 5], [6, 7]]


class Res:
    __slots__ = ("name", "w", "wl", "war", "rc", "rd", "ov", "lo", "hi", "excl")

    def __init__(self, name, lo=None, hi=None):
        self.name = name
        self.w = None
        self.wl = []
        self.war = []
        self.rc = {}
        self.rd = []
        self.ov = []
        self.lo = lo
        self.hi = hi
        self.excl = False

    def readers(self):
        return list(self.rc.values()) + self.rd

    def writers(self):
        return ([self.w] if self.w is not None else []) + self.wl


class Op:
    __slots__ = ("eng", "fn", "dma", "deps", "sig", "sem", "val", "idx", "inc", "ring")

    def __init__(self, eng, fn, dma, idx, ring, inc):
        self.eng = eng
        self.fn = fn
        self.dma = dma
        self.deps = []
        self.sig = dma
        self.sem = None
        self.val = 0
        self.idx = idx
        self.inc = inc
        self.ring = ring


class Prog:
    ENGS = ("pe", "act", "dve", "pool", "sp")

    def __init__(self, nc):
        self.nc = nc
        self.ops = {e: [] for e in self.ENGS}
        self.n = 0

    def res(self, name):
        return Res(name)

    def add(self, eng, fn, reads=(), writes=(), pwrites=(), dma=False, cc=False):
        asyn = dma or cc
        op = Op(eng, fn, asyn, self.n, ("cc" if cc else eng), (1 if cc else 16) if asyn else 1)
        self.n += 1
        deps = {}

        def dep(o):
            if o is None:
                return
            if (not o.dma) and (not asyn) and o.eng == eng and eng == "pe":
                return
            if o.dma:
                deps[("d", o.idx)] = o
            else:
                k = ("c", o.eng)
                if k not in deps or deps[k].idx < o.idx:
                    deps[k] = o

        def dep_all(q):
            for o in q.writers():
                dep(o)
            for o in q.readers():
                dep(o)
            for o in q.war:
                dep(o)

        for r in reads:
            for q in [r] + r.ov:
                for o in q.writers():
                    dep(o)
            if r.excl:
                for o in r.rc.values():
                    if o.eng != eng:
                        dep(o)
        for w in writes:
            for q in [w] + w.ov:
                dep_all(q)
        for w in pwrites:
            rs = w.readers()
            if rs:
                w.war = rs + w.wl + ([w.w] if w.w is not None else [])
                w.rc = {}
                w.rd = []
                w.wl = []
                w.w = None
            for o in w.war:
                dep(o)
            if w.w is not None:
                dep(w.w)
            for q in w.ov:
                dep_all(q)
        op.deps = list(deps.values())
        for o in op.deps:
            o.sig = True
        for r in reads:
            if asyn:
                r.rd.append(op)
            else:
                r.rc[eng] = op
        for w in writes:
            w.w = op
            w.wl = []
            w.war = []
            w.rc = {}
            w.rd = []
        for w in pwrites:
            w.wl.append(op)
        self.ops[eng].append(op)
        return op

    def emit(self):
        nc = self.nc
        esem = {e: nc.alloc_semaphore("sem_" + e) for e in self.ENGS}
        rings = {"sp": [nc.alloc_semaphore(f"dsp{i}") for i in range(16)],
                 "pool": [nc.alloc_semaphore(f"dpl{i}") for i in range(12)],
                 "act": [nc.alloc_semaphore(f"dac{i}") for i in range(4)],
                 "cc": [nc.alloc_semaphore(f"dcc{i}") for i in range(8)]}
        rk = {k: 0 for k in rings}
        rcnt = {}
        final = {}
        last_on_sem = {}
        prev_of = {}
        for e in self.ENGS:
            cnt = 0
            for op in self.ops[e]:
                if op.dma:
                    ring = rings[op.ring]
                    s = ring[rk[op.ring] % len(ring)]
                    rk[op.ring] += 1
                    rcnt[s] = rcnt.get(s, 0) + op.inc
                    op.sem = s
                    op.val = rcnt[s]
                    final[s] = rcnt[s]
                    prev_of[op.idx] = last_on_sem.get(s)
                    last_on_sem[s] = op
                elif op.sig:
                    cnt += 1
                    op.sem = esem[e]
                    op.val = cnt
        prog = self

        def run(e, eh, tail=False):
            waited = {}

            def w(sem, val):
                if waited.get(sem, 0) < val:
                    eh.wait_ge(sem, val)
                    waited[sem] = val

            import os
            dbg = os.environ.get("KDBG_EMIT")
            for op in prog.ops[e]:
                if dbg:
                    print("EMIT", e, op.idx, "line", op.fn.__code__.co_firstlineno, "dma" if op.dma else "c",
                          "sig" if op.sig else "-", getattr(op.sem, "name", op.sem), op.val,
                          "deps", [(d.eng, d.idx, getattr(d.sem, "name", d.sem), d.val) for d in op.deps])
                for d in op.deps:
                    w(d.sem, d.val)
                if op.dma:
                    p = prev_of[op.idx]
                    if p is not None:
                        w(p.sem, p.val)
                ins = op.fn(eh)
                if op.sig:
                    ins.then_inc(op.sem, op.inc)
            if tail:
                for s_, v in final.items():
                    w(s_, v)

        with nc.allow_non_contiguous_dma(reason="small strided vector loads"):
            with nc.Block() as block:
                @block.sync
                def _(eh):
                    run("sp", eh, tail=True)

                @block.gpsimd
                def _(eh):
                    run("pool", eh)

                @block.vector
                def _(eh):
                    run("dve", eh)

                @block.scalar
                def _(eh):
                    run("act", eh)

                @block.tensor
                def _(eh):
                    run("pe", eh)


class Arena:
    def __init__(self, nc, prog, nbytes):
        self.t = nc.alloc_sbuf_tensor("arena", [128, nbytes // 2], BF16)
        self.nbytes = nbytes
        self.prog = prog
        self.all = []

    def multi(self, name, lo, hi, n):
        rs = [Res(f"{name}{i}", lo, hi) for i in range(n)]
        for q in self.all:
            if q.lo < hi and lo < q.hi:
                for r in rs:
                    q.ov.append(r)
                    r.ov.append(q)
        self.all.extend(rs)
        return rs

    def buf(self, name, off, shape, dt, res=True):
        esz = 2 if dt == BF16 else 4
        n = 1
        for s in shape:
            n *= s
        nb = n * esz
        assert off % 4 == 0 and off + nb <= self.nbytes, (name, off, nb, self.nbytes)
        ap = self.t[:, off // 2:(off + nb) // 2]
        if dt != BF16:
            ap = ap.bitcast(dt)
        if len(shape) == 2:
            ap = ap.rearrange("p (a b) -> p a b", a=shape[0])
        elif len(shape) == 3:
            ap = ap.rearrange("p (a b c) -> p a b c", a=shape[0], b=shape[1])
        if not res:
            return ap, None
        r = Res(name, off, off + nb)
        for q in self.all:
            if q.lo < r.hi and r.lo < q.hi:
                q.ov.append(r)
                r.ov.append(q)
        self.all.append(r)
        return ap, r


W_BLK = {"uf": (0, 2), "zf": (2, 4), "q": (4, 6), "k": (6, 8), "v": (8, 10),
         "za": (10, 12), "gf": (12, 16), "ga": (16, 20)}


def build_program(n_layers=DEPTH, upto="MPVABXCDE"):
    nc = bass.Bass("TRN2", target_bir_lowering=False)
    pg = Prog(nc)

    used_inputs = []
    nc._used_inputs = used_inputs

    class _Lazy:
        def __init__(self, name, shape, dt):
            self.a = (name, shape, dt)
            self.t = None

        def ap(self):
            if self.t is None:
                self.t = nc.dram_tensor(self.a[0], self.a[1], self.a[2], kind="ExternalInput")
                used_inputs.append(self.a[0])
            return self.t.ap()

    def din(name, shape, dt=F32):
        return _Lazy(name, shape, dt)

    def dout(name, shape, dt=F32):
        return nc.dram_tensor(name, shape, dt, kind="ExternalOutput")

    def dscr(name, shape, dt=BF16):
        return nc.dram_tensor(name, shape, dt)

    xp = din("xp", [512, D])
    xs = din("xs", [2048, D])
    cvec = din("cvec", [2, D])
    NL = n_layers
    ck = din("ck", [NL, 512, 1024])
    cvv = din("cv", [NL, 512, 1024])
    w_in = din("w_in", [NL, D, 10240])
    w_fproj = din("w_fproj", [NL, 1024, D])
    w_aproj = din("w_aproj", [NL, 1024, D])
    w_out = din("w_out", [NL, D, D])
    w_mod = din("w_mod", [NL, D, 3 * D])
    b_mod = din("b_mod", [NL, 3 * D])
    g_norm = din("g_norm", [NL, D])
    g_q = din("g_q", [NL, 64])
    g_k = din("g_k", [NL, 64])
    g_sub = din("g_sub", [NL, 128])
    lam4 = din("lam4", [NL, 4, 64])
    c_cs = din("c_cs", [256, 512], BF16)
    c_csn = din("c_csn", [256, 512], BF16)
    c_dc = din("c_dc", [4096, 2048], BF16)
    c_ds = din("c_ds", [4096, 2048], BF16)
    c_rc = din("c_rc", [128, 2048])
    c_rs = din("c_rs", [128, 2048])
    c_idb = din("c_idb", [128, 128], BF16)
    c_idf = din("c_idf", [128, 128])
    c_blk = din("c_blk", [128, 128], BF16)
    c_prm = din("c_prm", [128, 128], BF16)

    yp = dout("yp", [512, D])
    ys = dout("ys", [2048, D])
    nk = dout("nk", [2, NL, 256, 1024])
    nv = dout("nv", [2, NL, 256, 1024])

    hT_d = dscr("hT_d", [KC, 128, NTOK])
    modrows = dscr("modrows", [NL, 2, 3 * D], F32)
    kcT_d = dscr("kcT_d", [NL, NH, 128, 512])
    kpT_d = dscr("kpT_d", [NH, 128, 512])
    vp_d = dscr("vp_d", [512, 1024])
    abp_d = dscr("abp_d", [512, 2048])
    yfT_d = dscr("yfT_d", [8, 128, NTOK])
    yaT_d = dscr("yaT_d", [8, 128, NTOK])
    kx_in = [[dscr(f"kxi{l}_{c}", [512, 2048]) for c in range(2)] for l in range(n_layers)]
    kx_out = [[dscr(f"kxo{l}_{c}", [1024, 2048]) for c in range(2)] for l in range(n_layers)]
    vx_in = [[dscr(f"vxi{l}_{c}", [1024, 1024]) for c in range(2)] for l in range(n_layers)]
    vx_out = [[dscr(f"vxo{l}_{c}", [2048, 1024]) for c in range(2)] for l in range(n_layers)]
    ax_in = [[dscr(f"axi{l}_{c}", [512, 2048]) for c in range(4)] for l in range(n_layers)]
    ax_out = [[dscr(f"axo{l}_{c}", [1024, 2048]) for c in range(4)] for l in range(n_layers)]

    R = pg.res
    r_yp = [R(f"yp{t}") for t in range(4)]
    r_ys = [R(f"ys{t}") for t in range(16)]
    r_hTd = [R(f"hTd{tb}") for tb in range(NTB)]
    r_mod = [R(f"modrows{l}") for l in range(DEPTH)]
    r_kcT = R("kcT")
    r_kpT = R("kpT")
    r_vp = R("vp")
    r_abp = R("abp")
    r_yfd = [R(f"yfd{tb}") for tb in range(NTB)]
    r_yad = [R(f"yad{tb}") for tb in range(NTB)]
    r_kxi = [[R(f"kxi{l}{c}") for c in range(2)] for l in range(n_layers)]
    r_kxo = [[R(f"kxo{l}{c}") for c in range(2)] for l in range(n_layers)]
    r_vxi = [[R(f"vxi{l}{c}") for c in range(2)] for l in range(n_layers)]
    r_vxo = [[R(f"vxo{l}{c}") for c in range(2)] for l in range(n_layers)]
    r_axi = [[R(f"axi{l}{c}") for c in range(4)] for l in range(n_layers)]
    r_axo = [[R(f"axo{l}{c}") for c in range(4)] for l in range(n_layers)]
    r_nk = R("nk")
    r_nv = R("nv")

    psum = [nc.alloc_psum_tensor(f"ps{i}", [128, 512], F32) for i in range(8)]
    r_ps = [R(f"ps{i}") for i in range(8)]
    for _q in r_ps:
        _q.excl = True

    ar = Arena(nc, pg, 206 * 1024)
    off = [0]

    def pers(name, shape, dt):
        esz = 2 if dt == BF16 else 4
        n = esz
        for s in shape:
            n *= s
        n = (n + 63) // 64 * 64
        o = off[0]
        off[0] += n
        return ar.buf(name, o, shape, dt)

    identb, r_identb = pers("identb", [128], BF16)
    identf, r_identf = pers("identf", [128], F32)
    onesb, r_onesb = pers("onesb", [128], BF16)
    onesf, r_onesf = pers("onesf", [128], F32)
    blk64, r_blk64 = pers("blk64", [128], BF16)
    permb, r_permb = pers("permb", [128], BF16)
    cs256, r_cs256 = pers("cs256", [2, 512], BF16)
    csn256, r_csn256 = pers("csn256", [2, 512], BF16)
    epsc, r_epsc = pers("epsc", [1], F32)
    scol, r_scol = pers("scol", [KC, 2], BF16)
    craw, r_craw = pers("craw", [KC, 2], F32)
    geffT = [pers(f"geffT{c}", [KC], F32) for c in range(2)]
    shiftT = [pers(f"shiftT{c}", [KC], F32) for c in range(2)]
    scaleT = [pers(f"scaleT{c}", [KC], F32) for c in range(2)]
    gnT, r_gnT = pers("gnT", [KC], F32)
    gqc, r_gqc = pers("gqc", [1], F32)
    gkc, r_gkc = pers("gkc", [1], F32)
    gsc, r_gsc = pers("gsc", [1], F32)
    lamv, r_lamv = pers("lamv", [4, 64], F32)
    lamp, r_lamp = pers("lamp", [2, 64], F32)
    lams, r_lams = pers("lams", [4], F32)
    nlam, r_nlam = pers("nlam", [1], F32)
    ssq = [pers(f"ssq{i}", [4], F32) for i in range(2)]
    qT_off = off[0]
    qT_all, r_qT = pers("qT_all", [NH, NTOK], BF16)
    r_qTh = r_qT
    X0 = off[0]
    XSZ = 80 * 1024
    S0 = X0 + XSZ
    SSZ = 206 * 1024 - S0
    assert SSZ >= 72 * 1024, SSZ

    def cload(dst, rdst, src_ap):
        pg.add("sp", lambda e: e.dma_start(out=dst, in_=src_ap), writes=[rdst], dma=True)

    cload(identb, r_identb, c_idb.ap())
    cload(identf, r_identf, c_idf.ap())
    cload(blk64, r_blk64, c_blk.ap())
    cload(permb, r_permb, c_prm.ap())
    cload(cs256, r_cs256, c_cs.ap().rearrange("(k p) n -> p k n", p=128))
    cload(csn256, r_csn256, c_csn.ap().rearrange("(k p) n -> p k n", p=128))
    pg.add("dve", lambda e: e.memset(onesb, 1.0), writes=[r_onesb])
    pg.add("dve", lambda e: e.memset(onesf, 1.0), writes=[r_onesf])
    pg.add("dve", lambda e: e.memset(epsc, EPS), writes=[r_epsc])

    def mm_group(ps_i, pairs, extra_reads, n=512, m=128):
        def fn(e):
            ins = None
            k = len(pairs)
            for i, (a, b) in enumerate(pairs):
                ins = e.matmul(psum[ps_i][0:m, 0:n], a, b, start=(i == 0), stop=(i == k - 1))
            return ins
        return pg.add("pe", fn, reads=extra_reads, writes=[r_ps[ps_i]])

    bank_rr = {"main": [0, [0, 1, 2, 3]], "aux": [0, [4, 5]], "aux2": [0, [6, 7]]}

    def nbank(kind):
        st = bank_rr[kind]
        b = st[1][st[0] % len(st[1])]
        st[0] += 1
        return b

    class Slots:
        def __init__(self, name, base, n, shape, dt):
            esz = 2 if dt == BF16 else 4
            nb = esz
            for s in shape:
                nb *= s
            nb = (nb + 63) // 64 * 64
            self.items = [ar.buf(f"{name}{i}", base + i * nb, shape, dt) for i in range(n)]
            self.k = 0
            self.end = base + n * nb

        def next(self):
            it = self.items[self.k % len(self.items)]
            self.k += 1
            return it

    def stage_M():
        o = S0
        wsl = Slots("m_w", o, 2, [KC, 512], BF16); o = wsl.end
        brow, r_brow = ar.buf("m_brow", o, [3 * D], F32); o += 3 * D * 4
        mrow = Slots("m_mrow", o, 2, [512], F32); o = mrow.end
        assert o <= S0 + SSZ
        for cv_i in range(2):
            src = cvec.ap()[cv_i:cv_i + 1, :].rearrange("o (k p) -> p (o k)", p=128)
            pg.add("sp", lambda e, src=src, cv_i=cv_i: e.dma_start(out=craw[:, :, cv_i], in_=src),
                   pwrites=[r_craw], dma=True)
        pg.add("act", lambda e: e.activation(out=scol, in_=craw, func=AF.Silu), reads=[r_craw], writes=[r_scol])
        for l in range(n_layers):
            for i in range(2):
                pg.add("sp", lambda e, l=l, i=i: e.dma_start(out=brow[i:i + 1, :], in_=b_mod.ap()[l:l + 1, :]),
                       pwrites=[r_brow], dma=True)
            for cb in range(12):
                wt, r_wt = wsl.next()
                src = w_mod.ap()[l, :, cb * 512:(cb + 1) * 512].rearrange("(k p) n -> p k n", p=128)
                pg.add("pool", lambda e, wt=wt, src=src: e.dma_start(out=wt, in_=src), writes=[r_wt], dma=True)
                b = nbank("main")
                mm_group(b, [(scol[:, kc, :], wt[:, kc, :]) for kc in range(KC)], [r_scol, r_wt], n=512, m=2)
                mr, r_mr = mrow.next()
                pg.add("dve", lambda e, mr=mr, b=b, cb=cb: e.tensor_tensor(
                    out=mr[0:2, :], in0=psum[b][0:2, :], in1=brow[0:2, cb * 512:(cb + 1) * 512], op=ALU.add),
                    reads=[r_ps[b], r_brow], writes=[r_mr])
                pg.add("sp", lambda e, mr=mr, l=l, cb=cb: e.dma_start(
                    out=modrows.ap()[l, :, cb * 512:(cb + 1) * 512], in_=mr[0:2, :]),
                    reads=[r_mr], pwrites=[r_mod[l]], dma=True)

    def stage_P():
        o = S0
        ct = Slots("p_ct", o, 2, [1024], F32); o = ct.end
        cb16 = Slots("p_cb", o, 2, [1024], BF16); o = cb16.end
        co = Slots("p_co", o, 2, [NH, 128], BF16); o = co.end
        for l in range(n_layers):
            for t in range(4):
                c_t, r_ct = ct.next()
                pg.add("sp", lambda e, c_t=c_t, l=l, t=t: e.dma_start(out=c_t, in_=ck.ap()[l, t * 128:(t + 1) * 128, :]),
                       writes=[r_ct], dma=True)
                c_b, r_cb = cb16.next()
                pg.add("dve", lambda e, c_b=c_b, c_t=c_t: e.tensor_copy(out=c_b, in_=c_t), reads=[r_ct], writes=[r_cb])
                c_o, r_co = co.next()
                for hg in range(2):
                    b = nbank("main")

                    def fn(e, c_b=c_b, b=b, hg=hg):
                        ins = None
                        for hh in range(4):
                            h = hg * 4 + hh
                            ins = e.matmul(psum[b][:, hh * 128:(hh + 1) * 128], c_b[:, h * 128:(h + 1) * 128], identb,
                                           start=True, stop=True)
                        return ins
                    pg.add("pe", fn, reads=[r_cb, r_identb], writes=[r_ps[b]])
                    pg.add("act", lambda e, c_o=c_o, b=b, hg=hg: e.activation(
                        out=c_o[:, hg * 4:(hg + 1) * 4, :], in_=psum[b][:, :].rearrange("p (h n) -> p h n", h=4), func=AF.Copy),
                        reads=[r_ps[b]], pwrites=[r_co])
                pg.add("sp", lambda e, c_o=c_o, l=l, t=t: e.dma_start(
                    out=kcT_d.ap()[l, :, :, t * 128:(t + 1) * 128].rearrange("h p n -> p h n"), in_=c_o),
                    reads=[r_co], pwrites=[r_kcT], dma=True)

    def stage_V(l):
        lam_init = 0.8 - 0.6 * math.exp(-0.3 * l)
        pg.add("sp", lambda e: e.dma_start(out=gnT, in_=g_norm.ap()[l:l + 1, :].rearrange("o (k p) -> p (o k)", p=128)),
               writes=[r_gnT], dma=True)
        for cv_i in range(2):
            for (dst, rdst), lo in ((shiftT[cv_i], 0), (scaleT[cv_i], D)):
                src = modrows.ap()[l, cv_i:cv_i + 1, lo:lo + D].rearrange("o (k p) -> p (o k)", p=128)
                pg.add("sp", lambda e, dst=dst, src=src: e.dma_start(out=dst, in_=src),
                       reads=[r_mod[l]], writes=[rdst], dma=True)
            ge, r_ge = geffT[cv_i]
            sc, r_sc = scaleT[cv_i]
            pg.add("dve", lambda e, ge=ge, sc=sc: e.scalar_tensor_tensor(
                out=ge, in0=sc, scalar=1.0, in1=gnT, op0=ALU.add, op1=ALU.mult),
                reads=[r_sc, r_gnT], writes=[r_ge])
        for (dst, rdst, src_t, n) in ((gqc, r_gqc, g_q, 64), (gkc, r_gkc, g_k, 64)):
            for half in range(2):
                src = src_t.ap()[l:l + 1, :].rearrange("o n -> n o")
                pg.add("sp", lambda e, dst=dst, src=src, half=half: e.dma_start(out=dst[half * 64:(half + 1) * 64, :], in_=src),
                       pwrites=[rdst], dma=True)
        pg.add("sp", lambda e: e.dma_start(out=gsc, in_=g_sub.ap()[l:l + 1, :].rearrange("o n -> n o")),
               writes=[r_gsc], dma=True)
        pg.add("dve", lambda e: e.tensor_scalar(out=gsc, in0=gsc, scalar1=float(1.0 - lam_init), scalar2=None, op0=ALU.mult),
               reads=[r_gsc], writes=[r_gsc])
        pg.add("sp", lambda e: e.dma_start(out=lamv, in_=lam4.ap()[l:l + 1, :, :].broadcast_to([128, 4, 64])),
               writes=[r_lamv], dma=True)
        pg.add("dve", lambda e: e.tensor_tensor(out=lamp[:, 0, :], in0=lamv[:, 0, :], in1=lamv[:, 1, :], op=ALU.mult),
               reads=[r_lamv], pwrites=[r_lamp])
        pg.add("dve", lambda e: e.tensor_tensor(out=lamp[:, 1, :], in0=lamv[:, 2, :], in1=lamv[:, 3, :], op=ALU.mult),
               reads=[r_lamv], pwrites=[r_lamp])
        pg.add("dve", lambda e: e.tensor_reduce(out=lams[:, 0:2], in_=lamp, axis=AX.X, op=ALU.add),
               reads=[r_lamp], writes=[r_lams])
        pg.add("act", lambda e: e.activation(out=lams[:, 2:4], in_=lams[:, 0:2], func=AF.Exp),
               reads=[r_lams], writes=[r_lams])
        pg.add("dve", lambda e: e.scalar_tensor_tensor(
            out=nlam, in0=lams[:, 3:4], scalar=float(-lam_init), in1=lams[:, 2:3], op0=ALU.add, op1=ALU.subtract),
            reads=[r_lams], writes=[r_nlam])

    hT_all, _ = ar.buf("hT_all", X0, [KC, NTOK], BF16, res=False)
    _r = ar.multi("hTt", X0, X0 + KC * NTOK * 2, 40)
    r_hTt = [[_r[2 * t], _r[2 * t + 1]] for t in range(20)]

    def hT_tb(tb):
        out = []
        for i in range(4):
            out += r_hTt[tb * 4 + i]
        return out

    def x_src(l, t):
        if t < 4:
            t_, r_ = (xp if l == 0 else yp), r_yp[t]
            return t_.ap()[t * 128:(t + 1) * 128, :], r_
        t2 = t - 4
        t_, r_ = (xs if l == 0 else ys), r_ys[t2]
        return t_.ap()[t2 * 128:(t2 + 1) * 128, :], r_

    def stage_A(l):
        o = S0
        xt = Slots("a_xt", o, 2, [D], F32); o = xt.end
        xnb = Slots("a_xn", o, 2, [D], BF16); o = xnb.end
        junk, r_junk = ar.buf("a_junk", o, [D], BF16); o += D * 2
        for t in range(20):
            cv_i = 0 if t < 4 else 1
            x_t, r_xt = xt.next()
            src, r_src = x_src(l, t)
            pg.add("sp", lambda e, x_t=x_t, src=src: e.dma_start(out=x_t, in_=src),
                   reads=([r_src] if l > 0 else []), writes=[r_xt], dma=True)
            sq, r_sq = ssq[t % 2]
            pg.add("dve", lambda e, sq=sq: e.memset(sq, 0.0), writes=[r_sq])
            pg.add("act", lambda e, x_t=x_t, sq=sq: e.activation(out=junk, in_=x_t, func=AF.Square, accum_out=sq[:, 0:1]),
                   reads=[r_xt], writes=[r_junk, r_sq])
            pg.add("act", lambda e, sq=sq: e.activation(out=sq[:, 1:2], in_=sq[:, 0:1], func=AF.Ln, scale=1.0 / D, bias=epsc),
                   reads=[r_sq, r_epsc], writes=[r_sq])
            pg.add("act", lambda e, sq=sq: e.activation(out=sq[:, 2:3], in_=sq[:, 1:2], func=AF.Exp, scale=-0.5),
                   reads=[r_sq], writes=[r_sq])
            xn, r_xn = xnb.next()
            pg.add("dve", lambda e, xn=xn, x_t=x_t, sq=sq: e.tensor_scalar(
                out=xn, in0=x_t, scalar1=sq[:, 2:3], scalar2=None, op0=ALU.mult),
                reads=[r_xt, r_sq], writes=[r_xn])
            ge, r_ge = geffT[cv_i]
            sh, r_sh = shiftT[cv_i]
            for q4 in range(4):
                b = nbank("main")

                def fn(e, xn=xn, b=b, q4=q4):
                    ins = None
                    for j in range(4):
                        kc = q4 * 4 + j
                        ins = e.matmul(psum[b][:, j * 128:(j + 1) * 128], xn[:, kc * 128:(kc + 1) * 128], identb,
                                       start=True, stop=True)
                    return ins
                pg.add("pe", fn, reads=[r_xn, r_identb], writes=[r_ps[b]])
                for j in range(4):
                    kc = q4 * 4 + j
                    dst = hT_all[:, kc, t * 128:(t + 1) * 128]
                    srcp = psum[b][:, j * 128:(j + 1) * 128]
                    if q4 % 2 == 0:
                        pg.add("dve", lambda e, dst=dst, srcp=srcp, ge=ge, sh=sh, kc=kc: e.tensor_scalar(
                            out=dst, in0=srcp, scalar1=ge[:, kc:kc + 1], scalar2=sh[:, kc:kc + 1],
                            op0=ALU.mult, op1=ALU.add),
                            reads=[r_ps[b], r_ge, r_sh], pwrites=[r_hTt[t][0]])
                    else:
                        pg.add("act", lambda e, dst=dst, srcp=srcp, ge=ge, sh=sh, kc=kc: e.activation(
                            out=dst, in_=srcp, func=AF.Identity, scale=ge[:, kc:kc + 1], bias=sh[:, kc:kc + 1]),
                            reads=[r_ps[b], r_ge, r_sh], pwrites=[r_hTt[t][1]])
        for tb in range(NTB):
            pg.add("sp", lambda e, tb=tb: e.dma_start(
                out=hT_d.ap()[:, :, tb * 512:(tb + 1) * 512].rearrange("k p n -> p k n"),
                in_=hT_all[:, :, tb * 512:(tb + 1) * 512]),
                reads=hT_tb(tb), writes=[r_hTd[tb]], dma=True)

    def stage_B(l):
        o = S0
        wsl = Slots("b_w", o, 2, [KC, 512], BF16); o = wsl.end
        cosT, r_cos = ar.buf("b_cos", o, [2048], F32); o += 8192
        sinT, r_sin = ar.buf("b_sin", o, [2048], F32); o += 8192
        sqb = Slots("b_sq", o, 1, [512], BF16); o = sqb.end
        lnv = Slots("b_ln", o, 1, [512], F32); o = lnv.end
        qn = Slots("b_qn", o, 3, [512], F32); o = qn.end
        qnb = Slots("b_qnb", o, 2, [512], BF16); o = qnb.end
        t1s = Slots("b_t1", o, 1, [512], F32); o = t1s.end
        t2s = Slots("b_t2", o, 1, [512], F32); o = t2s.end
        outb = Slots("b_ob", o, 1, [512], BF16); o = outb.end
        uT = Slots("b_uT", o, 1, [2, 512], BF16); o = uT.end
        abst = Slots("b_ab", o, 1, [4, 512], BF16); o = abst.end
        nkst = Slots("b_nk", o, 1, [4, 128], F32); o = nkst.end
        vst = Slots("b_vs", o, 2, [512], BF16); o = vst.end
        vstf = Slots("b_vf", o, 1, [512], F32); o = vstf.end
        assert o <= S0 + SSZ, (o, S0 + SSZ)
        pg.add("sp", lambda e: e.dma_start(out=cosT, in_=c_rc.ap()), writes=[r_cos], dma=True)
        pg.add("sp", lambda e: e.dma_start(out=sinT, in_=c_rs.ap()), writes=[r_sin], dma=True)

        def load_w(blk):
            wt, r_wt = wsl.next()
            src = w_in.ap()[l, :, blk * 512:(blk + 1) * 512].rearrange("(k p) n -> p k n", p=128)
            pg.add("pool", lambda e: e.dma_start(out=wt, in_=src), writes=[r_wt], dma=True)
            return wt, r_wt

        import os
        _parts = os.environ.get("KDBG_B", "uf,q,k,v").split(",")
        for bi, blk in enumerate(range(*W_BLK["uf"]) if "uf" in _parts else []):
            wt, r_wt = load_w(blk)
            for gl in range(2):
                g = bi * 2 + gl
                for tb in range(NTB):
                    u_t, r_ut = uT.next()
                    for cc in range(2):
                        b = nbank("main")
                        c0 = gl * 256 + cc * 128
                        mm_group(b, [(wt[:, kc, c0:c0 + 128], hT_all[:, kc, tb * 512:(tb + 1) * 512]) for kc in range(KC)],
                                 [r_wt] + hT_tb(tb))
                        if cc == 0:
                            pg.add("act", lambda e, u_t=u_t, b=b: e.activation(out=u_t[:, 0, :], in_=psum[b][:, :], func=AF.Copy),
                                   reads=[r_ps[b]], pwrites=[r_ut])
                        else:
                            pg.add("dve", lambda e, u_t=u_t, b=b: e.tensor_copy(out=u_t[:, 1, :], in_=psum[b][:, :]),
                                   reads=[r_ps[b]], pwrites=[r_ut])
                    ab, r_ab = abst.next()
                    for tt in range(4):
                        b = nbank("aux")
                        mm_group(b, [(u_t[:, k2, tt * 128:(tt + 1) * 128], cs256[:, k2, :]) for k2 in range(2)],
                                 [r_ut, r_cs256])
                        if tt % 2 == 0:
                            pg.add("act", lambda e, ab=ab, b=b, tt=tt: e.activation(out=ab[:, tt, :], in_=psum[b][:, :], func=AF.Copy),
                                   reads=[r_ps[b]], pwrites=[r_ab])
                        else:
                            pg.add("dve", lambda e, ab=ab, b=b, tt=tt: e.tensor_copy(out=ab[:, tt, :], in_=psum[b][:, :]),
                                   reads=[r_ps[b]], pwrites=[r_ab])
                    if tb == 0:
                        dst = abp_d.ap()[:, g * 512:(g + 1) * 512].rearrange("(t p) n -> p t n", p=128)
                        rd = r_abp
                    else:
                        c = tb - 1
                        dst = ax_in[l][c].ap()[:, g * 512:(g + 1) * 512].rearrange("(t p) n -> p t n", p=128)
                        rd = r_axi[l][c]
                    pg.add("sp", lambda e, dst=dst, ab=ab: e.dma_start(out=dst, in_=ab), reads=[r_ab], pwrites=[rd], dma=True)

        def make_unit(which, h, tb, b):
            gcol, r_gcol = (gqc, r_gqc) if which == "q" else (gkc, r_gkc)
            st = {}

            def ph1():
                sq_t, r_sq = sqb.next()
                pg.add("act", lambda e: e.activation(out=sq_t, in_=psum[b][:, :], func=AF.Square),
                       reads=[r_ps[b]], writes=[r_sq])
                b2 = nbank("aux")
                mm_group(b2, [(blk64, sq_t)], [r_blk64, r_sq])
                ln_t, r_ln = lnv.next()
                pg.add("act", lambda e: e.activation(out=ln_t, in_=psum[b2][:, :], func=AF.Ln, scale=1.0 / 64, bias=epsc),
                       reads=[r_ps[b2], r_epsc], writes=[r_ln])
                pg.add("act", lambda e: e.activation(out=ln_t, in_=ln_t, func=AF.Exp, scale=-0.5),
                       reads=[r_ln], writes=[r_ln])
                qn_t, r_qn = qn.next()
                pg.add("dve", lambda e: e.scalar_tensor_tensor(
                    out=qn_t, in0=psum[b][:, :], scalar=gcol[:, 0:1], in1=ln_t, op0=ALU.mult, op1=ALU.mult),
                    reads=[r_ps[b], r_gcol, r_ln], writes=[r_qn])
                st["qn"] = (qn_t, r_qn)
                if tb > 0:
                    qb_t, r_qb = qnb.next()
                    pg.add("dve", lambda e: e.tensor_copy(out=qb_t, in_=qn_t), reads=[r_qn], writes=[r_qb])
                    st["qb"] = (qb_t, r_qb)

            def ph2():
                qn_t, r_qn = st["qn"]
                ob, r_ob = outb.next()
                if tb == 0:
                    pg.add("dve", lambda e: e.tensor_copy(out=ob, in_=qn_t), reads=[r_qn], writes=[r_ob])
                else:
                    qb_t, r_qb = st["qb"]
                    b3 = nbank("aux2")
                    mm_group(b3, [(permb, qb_t)], [r_permb, r_qb])
                    t1, r_t1 = t1s.next()
                    t2, r_t2 = t2s.next()
                    cs = (tb - 1) * 512
                    pg.add("dve", lambda e: e.tensor_tensor(out=t1, in0=qn_t, in1=cosT[:, cs:cs + 512], op=ALU.mult),
                           reads=[r_qn, r_cos], writes=[r_t1])
                    pg.add("dve", lambda e: e.tensor_tensor(out=t2, in0=psum[b3][:, :], in1=sinT[:, cs:cs + 512], op=ALU.mult),
                           reads=[r_ps[b3], r_sin], writes=[r_t2])
                    pg.add("dve", lambda e: e.tensor_tensor(out=ob, in0=t1, in1=t2, op=ALU.add),
                           reads=[r_t1, r_t2], writes=[r_ob])
                if which == "q":
                    pg.add("act", lambda e: e.activation(
                        out=qT_all[:, h, tb * 512:(tb + 1) * 512], in_=ob, func=AF.Copy),
                        reads=[r_ob], pwrites=[r_qT])
                elif tb == 0:
                    qf, r_qf = qn_t, r_qn
                    pg.add("sp", lambda e: e.dma_start(out=kpT_d.ap()[h, :, :], in_=ob),
                           reads=[r_ob], pwrites=[r_kpT], dma=True)
                    b4 = nbank("aux2")

                    def fn(e):
                        ins = None
                        for tt in range(4):
                            ins = e.matmul(psum[b4][:, tt * 128:(tt + 1) * 128], qf[:, tt * 128:(tt + 1) * 128], identf,
                                           start=True, stop=True)
                        return ins
                    pg.add("pe", fn, reads=[r_qf, r_identf], writes=[r_ps[b4]])
                    ns, r_ns = nkst.next()
                    pg.add("act", lambda e: e.activation(
                        out=ns, in_=psum[b4][:, :].rearrange("p (t n) -> p t n", t=4), func=AF.Copy),
                        reads=[r_ps[b4]], writes=[r_ns])
                    for s_ in range(2):
                        pg.add("sp", lambda e, s_=s_: e.dma_start(
                            out=nk.ap()[s_, l, :, h * 128:(h + 1) * 128].rearrange("(t p) n -> p t n", p=128),
                            in_=ns[:, s_ * 2:(s_ + 1) * 2, :]),
                            reads=[r_ns], dma=True)
                else:
                    c = h // 4
                    pg.add("sp", lambda e: e.dma_start(
                        out=kx_in[l][c].ap()[(h % 4) * 128:(h % 4 + 1) * 128, (tb - 1) * 512:tb * 512], in_=ob),
                        reads=[r_ob], pwrites=[r_kxi[l][c]], dma=True)
            return ph1, ph2

        inflight = []

        def advance():
            if len(inflight) >= 2 and inflight[-2][0] is not None:
                inflight[-2][0]()
                inflight[-2][0] = None
            if len(inflight) >= 3:
                u = inflight.pop(0)
                u[1]()

        for which in ("q", "k"):
            for bi, blk in enumerate(range(*W_BLK[which]) if which in _parts else []):
                wt, r_wt = load_w(blk)
                for hl in range(4):
                    h = bi * 4 + hl
                    for tb in range(NTB):
                        b = nbank("main")
                        mm_group(b, [(wt[:, kc, hl * 128:(hl + 1) * 128], hT_all[:, kc, tb * 512:(tb + 1) * 512])
                                     for kc in range(KC)], [r_wt] + hT_tb(tb))
                        inflight.append(list(make_unit(which, h, tb, b)))
                        advance()
        while inflight:
            u = inflight.pop(0)
            if u[0] is not None:
                u[0]()
            u[1]()

        for bi, blk in enumerate(range(*W_BLK["v"]) if "v" in _parts else []):
            wt, r_wt = load_w(blk)
            for t in range(20):
                b = nbank("main")
                mm_group(b, [(hT_all[:, kc, t * 128:(t + 1) * 128], wt[:, kc, :]) for kc in range(KC)], [r_wt] + r_hTt[t])
                vs, r_vs = vst.next()
                pg.add("act", lambda e, vs=vs, b=b: e.activation(out=vs, in_=psum[b][:, :], func=AF.Copy),
                       reads=[r_ps[b]], writes=[r_vs])
                if t < 4:
                    pg.add("sp", lambda e, vs=vs, t=t, bi=bi: e.dma_start(
                        out=vp_d.ap()[t * 128:(t + 1) * 128, bi * 512:(bi + 1) * 512], in_=vs),
                        reads=[r_vs], pwrites=[r_vp], dma=True)
                    vf, r_vf = vstf.next()
                    pg.add("dve", lambda e, vf=vf, b=b: e.tensor_copy(out=vf, in_=psum[b][:, :]),
                           reads=[r_ps[b]], writes=[r_vf])
                    s, tt = t // 2, t % 2
                    pg.add("sp", lambda e, vf=vf, s=s, tt=tt, bi=bi: e.dma_start(
                        out=nv.ap()[s, l, tt * 128:(tt + 1) * 128, bi * 512:(bi + 1) * 512], in_=vf),
                        reads=[r_vf], dma=True)
                else:
                    t2 = t - 4
                    c, tl = t2 // 8, t2 % 8
                    pg.add("sp", lambda e, vs=vs, c=c, tl=tl, bi=bi: e.dma_start(
                        out=vx_in[l][c].ap()[tl * 128:(tl + 1) * 128, bi * 512:(bi + 1) * 512], in_=vs),
                        reads=[r_vs], pwrites=[r_vxi[l][c]], dma=True)

    def stage_X(l):
        def ag(src, dst, r_src, r_dst):
            pg.add("pool", lambda e: e.collective_compute(
                "AllGather", ALU.bypass, replica_groups=PAIRS, ins=[src.ap()], outs=[dst.ap()]),
                reads=[r_src], writes=[r_dst], cc=True)
        for c in range(2):
            ag(kx_in[l][c], kx_out[l][c], r_kxi[l][c], r_kxo[l][c])
        for c in range(2):
            ag(vx_in[l][c], vx_out[l][c], r_vxi[l][c], r_vxo[l][c])
        for c in range(4):
            ag(ax_in[l][c], ax_out[l][c], r_axi[l][c], r_axo[l][c])

    def stage_C(l):
        o = X0
        kTs = Slots("c_kT", o, 2, [4608], BF16); o = kTs.end
        vSs = Slots("c_v", o, 2, [36, 128], BF16); o = vSs.end
        pS = Slots("c_p", o, 5, [512], BF16); o = pS.end
        rc0 = Slots("c_r", o, 2, [512], F32); o = rc0.end
        tS = Slots("c_t", o, 2, [512], F32); o = tS.end
        aS = Slots("c_a", o, 2, [512], F32); o = aS.end
        sqS = Slots("c_sq", o, 2, [512], BF16); o = sqS.end
        lnS = Slots("c_ln", o, 2, [512], F32); o = lnS.end
        _acc = Slots("c_acc", o, 8, [512], F32); o = _acc.end
        smS = Slots("c_sm", o, 2, [512], BF16); o = smS.end
        accS = [[[_acc.items[par * 4 + c * 2 + k] for k in range(2)] for c in range(2)] for par in range(2)]
        obS = Slots("c_ob", o, 2, [512], BF16); o = obS.end
        assert o <= X0 + XSZ, (o, X0 + XSZ)
        o = S0
        kpa, r_kpa = ar.buf("c_kpa", o, [NH, 512], BF16); o += NH * 512 * 2
        vpa, r_vpa = ar.buf("c_vpa", o, [4, 1024], BF16); o += 4 * 1024 * 2

        sbank = [0]
        qblk = [0]
        pending = []

        def run_pending(n=1):
            for _ in range(n):
                if pending:
                    pending.pop(0)()

        def core(h, q_ap, nq, ktiles, kv_reads, out_dst, r_out):
            nk_ = len(ktiles)
            par = qblk[0] % 2
            qblk[0] += 1
            bo = (4, 5) if par == 0 else (6, 7)
            accs = [[accS[par][c][0], accS[par][c][1]] for c in range(2)]

            def s_step(i):
                kT, _ = ktiles[i]
                pr = sbank[0] % 2
                sbank[0] += 1
                b0, b1 = 2 * pr, 2 * pr + 1

                def fn(e):
                    e.matmul(psum[b0][:, 0:nq], kT[0:64, :], q_ap[0:64, :], start=True, stop=True)
                    return e.matmul(psum[b1][:, 0:nq], kT[64:128, :], q_ap[64:128, :], start=True, stop=True)
                pg.add("pe", fn, reads=kv_reads + [r_qT], writes=[r_ps[b0], r_ps[b1]])
                ps_ = []
                for c, bb in ((0, b0), (1, b1)):
                    p_t, r_p = pS.next()
                    pg.add("act", lambda e, p_t=p_t, bb=bb: e.activation(
                        out=p_t[:, 0:nq], in_=psum[bb][:, 0:nq], func=AF.Exp, scale=0.125),
                        reads=[r_ps[bb]], writes=[r_p])
                    ps_.append((p_t, r_p))
                    a_ap, r_acc = accs[c][i % 2]
                    if i < 2:
                        pg.add("dve", lambda e, a_ap=a_ap, p_t=p_t: e.tensor_copy(out=a_ap[:, 0:nq], in_=p_t[:, 0:nq]),
                               reads=[r_p], writes=[r_acc])
                    else:
                        pg.add("dve", lambda e, a_ap=a_ap, p_t=p_t: e.tensor_tensor(
                            out=a_ap[:, 0:nq], in0=a_ap[:, 0:nq], in1=p_t[:, 0:nq], op=ALU.add),
                            reads=[r_p, r_acc], writes=[r_acc])
                return ps_

            def o_step(i, ps_):
                _, v = ktiles[i]
                (p0, r_p0), (p1, r_p1) = ps_

                def fn(e):
                    st, sp_ = (i == 0), (i == nk_ - 1)
                    e.matmul(psum[bo[0]][:, 0:nq], v, p0[:, 0:nq], start=st, stop=sp_)
                    return e.matmul(psum[bo[1]][:, 0:nq], v, p1[:, 0:nq], start=st, stop=sp_)
                pg.add("pe", fn, reads=kv_reads + [r_p0, r_p1], writes=[r_ps[bo[0]], r_ps[bo[1]]])

            prev = s_step(0)
            for i in range(nk_):
                nxt = s_step(i + 1) if i + 1 < nk_ else None
                o_step(i, prev)
                prev = nxt
                if i == 1 or i == 3:
                    run_pending()

            st_ = {}

            def epi1():
                pr = sbank[0] % 2
                sbank[0] += 1
                bl = (2 * pr, 2 * pr + 1)
                ts_ = []
                for c in range(2):
                    (a0, r_a0), (a1, r_a1) = accs[c]
                    sm, r_sm = smS.next()
                    pg.add("dve", lambda e, sm=sm, a0=a0, a1=a1: e.tensor_tensor(
                        out=sm[:, 0:nq], in0=a0[:, 0:nq], in1=a1[:, 0:nq], op=ALU.add),
                        reads=[r_a0, r_a1], writes=[r_sm])
                    mm_group(bl[c], [(onesb, sm[:, 0:nq])], [r_onesb, r_sm], n=nq)
                    r_t, r_r = rc0.next()
                    pg.add("dve", lambda e, r_t=r_t, c=c: e.reciprocal(out=r_t[:, 0:nq], in_=psum[bl[c]][:, 0:nq]),
                           reads=[r_ps[bl[c]]], writes=[r_r])
                    t_t, r_tt = tS.next()
                    pg.add("dve", lambda e, t_t=t_t, r_t=r_t, c=c: e.tensor_tensor(
                        out=t_t[:, 0:nq], in0=psum[bo[c]][:, 0:nq], in1=r_t[:, 0:nq], op=ALU.mult),
                        reads=[r_ps[bo[c]], r_r], writes=[r_tt])
                    ts_.append((t_t, r_tt))
                a_t, r_a = aS.next()
                (t0, r_t0), (t1, r_t1) = ts_
                pg.add("dve", lambda e: e.scalar_tensor_tensor(
                    out=a_t[:, 0:nq], in0=t1[:, 0:nq], scalar=nlam[:, 0:1], in1=t0[:, 0:nq], op0=ALU.mult, op1=ALU.add),
                    reads=[r_t0, r_t1, r_nlam], writes=[r_a])
                sq_t, r_sq = sqS.next()
                pg.add("act", lambda e: e.activation(out=sq_t[:, 0:nq], in_=a_t[:, 0:nq], func=AF.Square),
                       reads=[r_a], writes=[r_sq])
                st_["a"] = (a_t, r_a, sq_t, r_sq)

            def epi2():
                a_t, r_a, sq_t, r_sq = st_["a"]
                pr = sbank[0] % 2
                sbank[0] += 1
                bq = 2 * pr
                mm_group(bq, [(onesb, sq_t[:, 0:nq])], [r_onesb, r_sq], n=nq)
                ln_t, r_ln = lnS.next()
                pg.add("act", lambda e: e.activation(out=ln_t[:, 0:nq], in_=psum[bq][:, 0:nq], func=AF.Ln, scale=1.0 / 128, bias=epsc),
                       reads=[r_ps[bq], r_epsc], writes=[r_ln])
                pg.add("act", lambda e: e.activation(out=ln_t[:, 0:nq], in_=ln_t[:, 0:nq], func=AF.Exp, scale=-0.5),
                       reads=[r_ln], writes=[r_ln])
                ob, r_ob = obS.next()
                pg.add("dve", lambda e: e.scalar_tensor_tensor(
                    out=ob[:, 0:nq], in0=a_t[:, 0:nq], scalar=gsc[:, 0:1], in1=ln_t[:, 0:nq], op0=ALU.mult, op1=ALU.mult),
                    reads=[r_a, r_gsc, r_ln], writes=[r_ob])
                pg.add("sp", lambda e: e.dma_start(out=out_dst, in_=ob[:, 0:nq]), reads=[r_ob], pwrites=[r_out], dma=True)

            run_pending(len(pending))
            pending.append(epi1)
            pending.append(epi2)

        pg.add("sp", lambda e: e.dma_start(out=kpa, in_=kpT_d.ap().rearrange("h p n -> p h n")),
               reads=[r_kpT], writes=[r_kpa], dma=True)
        pg.add("sp", lambda e: e.dma_start(out=vpa, in_=vp_d.ap().rearrange("(t p) n -> p t n", p=128)),
               reads=[r_vp], writes=[r_vpa], dma=True)
        for s in range(2):
            for h in range(NH):
                kt = [(kpa[:, h, (s * 2 + i) * 128:(s * 2 + i + 1) * 128], vpa[:, s * 2 + i, h * 128:(h + 1) * 128])
                      for i in range(2)]
                core(h, qT_all[:, h, s * 256:(s + 1) * 256], 256, kt, [r_kpa, r_vpa],
                     yaT_d.ap()[h, :, s * 256:(s + 1) * 256], r_yad[0])

        def load_kv(h):
            kT, r_kT = kTs.next()
            vS, r_vS = vSs.next()
            pg.add("sp", lambda e: e.dma_start(out=kT[:, 0:512], in_=kcT_d.ap()[l, h, :, :]),
                   reads=[r_kcT], pwrites=[r_kT], dma=True)
            c, hl = h // 4, h % 4
            for r in range(2):
                pg.add("sp", lambda e, r=r: e.dma_start(
                    out=kT[:, 512 + r * 2048:512 + (r + 1) * 2048],
                    in_=kx_out[l][c].ap()[r * 512 + hl * 128:r * 512 + (hl + 1) * 128, :]),
                    reads=[r_kxo[l][c]], pwrites=[r_kT], dma=True)
            pg.add("pool", lambda e: e.dma_start(
                out=vS[:, 0:4, :], in_=cvv.ap()[l, :, h * 128:(h + 1) * 128].rearrange("(t p) n -> p t n", p=128)),
                pwrites=[r_vS], dma=True)
            for r in range(2):
                for c2 in range(2):
                    t0 = 4 + r * 16 + c2 * 8
                    pg.add("sp", lambda e, r=r, c2=c2, t0=t0: e.dma_start(
                        out=vS[:, t0:t0 + 8, :],
                        in_=vx_out[l][c2].ap()[r * 1024:(r + 1) * 1024, h * 128:(h + 1) * 128].rearrange("(t p) n -> p t n", p=128)),
                        reads=[r_vxo[l][c2]], pwrites=[r_vS], dma=True)
            return kT, r_kT, vS, r_vS

        nxt = load_kv(0)
        for h in range(NH):
            kT, r_kT, vS, r_vS = nxt
            if h + 1 < NH:
                nxt = load_kv(h + 1)
            kt = [(kT[:, i * 128:(i + 1) * 128], vS[:, i, :]) for i in range(36)]
            for qb in range(4):
                tb = 1 + qb
                core(h, qT_all[:, h, tb * 512:(tb + 1) * 512], 512, kt, [r_kT, r_vS],
                     yaT_d.ap()[h, :, tb * 512:(tb + 1) * 512], r_yad[tb])
        run_pending(len(pending))

    def stage_D(l):
        o = X0
        dC, r_dC = ar.buf("d_C", o, [32, 512], BF16); o += 32 * 512 * 2
        dS, r_dS = ar.buf("d_S", o, [32, 512], BF16); o += 32 * 512 * 2
        assert o <= X0 + XSZ
        o = S0
        abS = Slots("d_ab", o, 2, [2, 32, 128], BF16); o = abS.end
        abp, r_abpS = ar.buf("d_abp", o, [4, 2048], BF16); o += 4 * 2048 * 2
        yo = Slots("d_yo", o, 2, [512], BF16); o = yo.end
        assert o <= S0 + SSZ
        pg.add("sp", lambda e: e.dma_start(out=abp, in_=abp_d.ap().rearrange("(t p) n -> p t n", p=128)),
               reads=[r_abp], writes=[r_abpS], dma=True)
        for mc in range(8):
            g, cc = mc // 2, mc % 2
            b = nbank("main")
            for s in range(2):
                def fn(e, s=s, b=b, g=g, cc=cc):
                    ins = None
                    for i in range(2):
                        t = s * 2 + i
                        a_ap = abp[:, t, g * 512 + cc * 128:g * 512 + cc * 128 + 128]
                        b_ap = abp[:, t, g * 512 + 256 + cc * 128:g * 512 + 256 + cc * 128 + 128]
                        e.matmul(psum[b][:, s * 256:(s + 1) * 256], a_ap, csn256[:, i, 0:256], start=(i == 0), stop=False)
                        ins = e.matmul(psum[b][:, s * 256:(s + 1) * 256], b_ap, csn256[:, i, 256:512], start=False, stop=(i == 1))
                    return ins
                pg.add("pe", fn, reads=[r_abpS, r_csn256], writes=[r_ps[b]])
            y_t, r_y = yo.next()
            pg.add("act", lambda e, y_t=y_t, b=b: e.activation(out=y_t, in_=psum[b][:, :], func=AF.Copy),
                   reads=[r_ps[b]], writes=[r_y])
            pg.add("sp", lambda e, y_t=y_t, mc=mc: e.dma_start(out=yfT_d.ap()[mc, :, 0:512], in_=y_t),
                   reads=[r_y], pwrites=[r_yfd[0]], dma=True)
        for kb in range(4):
            pg.add("sp", lambda e, kb=kb: e.dma_start(
                out=dC, in_=c_dc.ap()[:, kb * 512:(kb + 1) * 512].rearrange("(t p) n -> p t n", p=128)),
                writes=[r_dC], dma=True)
            pg.add("sp", lambda e, kb=kb: e.dma_start(
                out=dS, in_=c_ds.ap()[:, kb * 512:(kb + 1) * 512].rearrange("(t p) n -> p t n", p=128)),
                writes=[r_dS], dma=True)
            for mc in range(8):
                g, cc = mc // 2, mc % 2
                ab, r_ab = abS.next()
                for c in range(4):
                    for r in range(2):
                        t0 = r * 16 + c * 4
                        for pl in range(2):
                            col = g * 512 + pl * 256 + cc * 128
                            pg.add("sp", lambda e, ab=ab, c=c, r=r, t0=t0, pl=pl, col=col: e.dma_start(
                                out=ab[:, pl, t0:t0 + 4, :],
                                in_=ax_out[l][c].ap()[r * 512:(r + 1) * 512, col:col + 128].rearrange("(t p) n -> p t n", p=128)),
                                reads=[r_axo[l][c]], pwrites=[r_ab], dma=True)
                b = nbank("main")
                pairs = []
                for t in range(32):
                    pairs.append((ab[:, 0, t, :], dC[:, t, :]))
                    pairs.append((ab[:, 1, t, :], dS[:, t, :]))
                mm_group(b, pairs, [r_ab, r_dC, r_dS])
                y_t, r_y = yo.next()
                pg.add("act", lambda e, y_t=y_t, b=b: e.activation(out=y_t, in_=psum[b][:, :], func=AF.Copy),
                       reads=[r_ps[b]], writes=[r_y])
                tb = 1 + kb
                pg.add("sp", lambda e, y_t=y_t, mc=mc, tb=tb: e.dma_start(
                    out=yfT_d.ap()[mc, :, tb * 512:(tb + 1) * 512], in_=y_t),
                    reads=[r_y], pwrites=[r_yfd[tb]], dma=True)

    def stage_E(l, last):
        o = X0
        hS, r_hS = ar.buf("e_h", o, [KC, 1024], BF16); o += KC * 1024 * 2
        mg, r_mg = ar.buf("e_mg", o, [KC, 1024], BF16); o += KC * 1024 * 2
        yfS, r_yfS = ar.buf("e_yf", o, [8, 1024], BF16); o += 8 * 1024 * 2
        assert o <= X0 + XSZ
        o = qT_off
        yaS, r_yaS = ar.buf("e_ya", o, [8, 1024], BF16); o += 8 * 1024 * 2
        gbc = [ar.buf(f"e_gate{c}", o + c * 8192, [D], F32) for c in range(2)]; o += 16384
        sgS = Slots("e_sg", o, 2, [512], F32); o = sgS.end
        assert o <= X0
        o = S0
        wsl = Slots("e_w", o, 2, [KC, 512], BF16); o = wsl.end
        wpj = Slots("e_wp", o, 2, [8, 512], BF16); o = wpj.end
        xq = Slots("e_xq", o, 4, [512], F32); o = xq.end
        xo = Slots("e_xo", o, 4, [512], F32); o = xo.end
        tmp = Slots("e_tmp", o, 2, [512], F32); o = tmp.end
        assert o <= S0 + SSZ, (o, S0 + SSZ)
        for cv_i in range(2):
            g_ap, r_g = gbc[cv_i]
            pg.add("sp", lambda e, g_ap=g_ap, cv_i=cv_i: e.dma_start(
                out=g_ap, in_=modrows.ap()[l, cv_i:cv_i + 1, 2 * D:3 * D].broadcast_to([128, D])),
                reads=[r_mod[l]], writes=[r_g], dma=True)

        def load_w(blk):
            wt, r_wt = wsl.next()
            src = w_in.ap()[l, :, blk * 512:(blk + 1) * 512].rearrange("(k p) n -> p k n", p=128)
            pg.add("pool", lambda e: e.dma_start(out=wt, in_=src), writes=[r_wt], dma=True)
            return wt, r_wt

        def load_wp(wten, fb):
            wt, r_wt = wpj.next()
            src = wten.ap()[l, :, fb * 512:(fb + 1) * 512].rearrange("(k p) n -> p k n", p=128)
            pg.add("pool", lambda e: e.dma_start(out=wt, in_=src), writes=[r_wt], dma=True)
            return wt, r_wt

        for sb in ([0, 1], [2, 3], [4]):
            t0 = sb[0] * 512
            nt = len(sb) * 512
            pg.add("sp", lambda e, t0=t0, nt=nt: e.dma_start(
                out=hS[:, :, 0:nt], in_=hT_d.ap()[:, :, t0:t0 + nt].rearrange("k p n -> p k n")),
                reads=[r_hTd[tb] for tb in sb], writes=[r_hS], dma=True)
            pg.add("sp", lambda e, t0=t0, nt=nt: e.dma_start(
                out=yfS[:, :, 0:nt], in_=yfT_d.ap()[:, :, t0:t0 + nt].rearrange("k p n -> p k n")),
                reads=[r_yfd[tb] for tb in sb], writes=[r_yfS], dma=True)
            pg.add("sp", lambda e, t0=t0, nt=nt: e.dma_start(
                out=yaS[:, :, 0:nt], in_=yaT_d.ap()[:, :, t0:t0 + nt].rearrange("k p n -> p k n")),
                reads=[r_yad[tb] for tb in sb], writes=[r_yaS], dma=True)
            for which, ybuf, r_yb in (("zf", yfS, r_yfS), ("za", yaS, r_yaS)):
                for bi, blk in enumerate(range(*W_BLK[which])):
                    wt, r_wt = load_w(blk)
                    for cc in range(4):
                        mc = bi * 4 + cc
                        for ti in range(len(sb)):
                            b = nbank("main")
                            mm_group(b, [(wt[:, kc, cc * 128:(cc + 1) * 128], hS[:, kc, ti * 512:(ti + 1) * 512])
                                         for kc in range(KC)], [r_wt, r_hS])
                            sg, r_sg = sgS.next()
                            pg.add("act", lambda e, sg=sg, b=b: e.activation(out=sg, in_=psum[b][:, :], func=AF.Silu),
                                   reads=[r_ps[b]], writes=[r_sg])
                            dst = ybuf[:, mc, ti * 512:(ti + 1) * 512]
                            pg.add("dve", lambda e, dst=dst, sg=sg: e.tensor_tensor(out=dst, in0=dst, in1=sg, op=ALU.mult),
                                   reads=[r_sg, r_yb], writes=[r_yb])
            for pi, (which, wten, ybuf, r_yb) in enumerate((("gf", w_fproj, yfS, r_yfS), ("ga", w_aproj, yaS, r_yaS))):
                for bi, blk in enumerate(range(*W_BLK[which])):
                    wt, r_wt = load_w(blk)
                    wp, r_wp = load_wp(wten, bi)
                    for cc in range(4):
                        fo = bi * 4 + cc
                        for ti in range(len(sb)):
                            b = nbank("main")
                            mm_group(b, [(wt[:, kc, cc * 128:(cc + 1) * 128], hS[:, kc, ti * 512:(ti + 1) * 512])
                                         for kc in range(KC)], [r_wt, r_hS])
                            sg, r_sg = sgS.next()
                            pg.add("act", lambda e, sg=sg, b=b: e.activation(out=sg, in_=psum[b][:, :], func=AF.Sigmoid),
                                   reads=[r_ps[b]], writes=[r_sg])
                            b2 = nbank("aux")
                            mm_group(b2, [(wp[:, j, cc * 128:(cc + 1) * 128], ybuf[:, j, ti * 512:(ti + 1) * 512])
                                          for j in range(8)], [r_wp, r_yb])
                            dst = mg[:, fo, ti * 512:(ti + 1) * 512]
                            if pi == 0:
                                pg.add("dve", lambda e, dst=dst, sg=sg, b2=b2: e.tensor_tensor(
                                    out=dst, in0=psum[b2][:, :], in1=sg, op=ALU.mult),
                                    reads=[r_ps[b2], r_sg], writes=[r_mg])
                            else:
                                tm, r_tm = tmp.next()
                                pg.add("dve", lambda e, tm=tm, sg=sg, b2=b2: e.tensor_tensor(
                                    out=tm, in0=psum[b2][:, :], in1=sg, op=ALU.mult),
                                    reads=[r_ps[b2], r_sg], writes=[r_tm])
                                pg.add("dve", lambda e, dst=dst, tm=tm: e.tensor_tensor(out=dst, in0=dst, in1=tm, op=ALU.add),
                                       reads=[r_tm, r_mg], writes=[r_mg])
            for fb in range(4):
                wt, r_wt = wsl.next()
                src = w_out.ap()[l, :, fb * 512:(fb + 1) * 512].rearrange("(k p) n -> p k n", p=128)
                pg.add("pool", lambda e, wt=wt, src=src: e.dma_start(out=wt, in_=src), writes=[r_wt], dma=True)
                for tl in range(nt // 128):
                    t = sb[0] * 4 + tl
                    cv_i = 0 if t < 4 else 1
                    xsrc, r_xsrc = x_src(l, t)
                    x_q, r_xq = xq.next()
                    pg.add("sp", lambda e, x_q=x_q, xsrc=xsrc, fb=fb: e.dma_start(out=x_q, in_=xsrc[:, fb * 512:(fb + 1) * 512]),
                           reads=([r_xsrc] if l > 0 else []), writes=[r_xq], dma=True)
                    b = nbank("main")
                    mm_group(b, [(mg[:, j, tl * 128:(tl + 1) * 128], wt[:, j, :]) for j in range(KC)], [r_mg, r_wt])
                    tm, r_tm = tmp.next()
                    g_ap, r_g = gbc[cv_i]
                    pg.add("dve", lambda e, tm=tm, b=b, g_ap=g_ap, fb=fb: e.tensor_tensor(
                        out=tm, in0=psum[b][:, :], in1=g_ap[:, fb * 512:(fb + 1) * 512], op=ALU.mult),
                        reads=[r_ps[b], r_g], writes=[r_tm])
                    x_o, r_xo = xo.next()
                    pg.add("dve", lambda e, x_o=x_o, tm=tm, x_q=x_q: e.tensor_tensor(out=x_o, in0=tm, in1=x_q, op=ALU.add),
                           reads=[r_tm, r_xq], writes=[r_xo])
                    if t < 4:
                        dst, r_dst = yp.ap()[t * 128:(t + 1) * 128, fb * 512:(fb + 1) * 512], r_yp[t]
                    else:
                        dst, r_dst = ys.ap()[(t - 4) * 128:(t - 3) * 128, fb * 512:(fb + 1) * 512], r_ys[t - 4]
                    pg.add("sp", lambda e, dst=dst, x_o=x_o: e.dma_start(out=dst, in_=x_o),
                           reads=[r_xo], pwrites=[r_dst], dma=True)

    stages = upto
    if "M" in stages:
        stage_M()
    if "P" in stages:
        stage_P()
    for l in range(n_layers):
        if "V" in stages:
            stage_V(l)
        if "A" in stages:
            stage_A(l)
        if "B" in stages:
            stage_B(l)
        if "X" in stages:
            stage_X(l)
        if "C" in stages:
            stage_C(l)
        if "D" in stages:
            stage_D(l)
        if "E" in stages:
            stage_E(l, l == n_layers - 1)
    pg.emit()
    return nc


def _consts(j):
    bf = ml_dtypes.bfloat16
    c = np.arange(256, dtype=np.float64)
    ang = 2.0 * np.pi * np.outer(c, c) / 256.0
    cs = np.concatenate([np.cos(ang), np.sin(ang)], axis=1) / 16.0
    csn = np.concatenate([np.cos(ang), -np.sin(ang)], axis=1) / 16.0
    n = np.arange(4096, dtype=np.float64)
    k = np.arange(2048 * j, 2048 * (j + 1), dtype=np.float64)
    nk_ = np.mod(np.outer(n, k), 4096.0)
    ang2 = 2.0 * np.pi * nk_ / 4096.0
    dc = np.cos(ang2) / 64.0
    ds = -np.sin(ang2) / 64.0
    tok = np.arange(2048 * j, 2048 * (j + 1))
    row = (tok // 64).astype(np.float32)
    col = (tok % 64).astype(np.float32)
    half = 32
    inv = (1.0 / (np.float32(10000.0) ** (np.arange(half // 2, dtype=np.float32) * np.float32(2.0) / np.float32(half)))).astype(np.float32)
    rc = np.zeros((128, 2048), np.float32)
    rs = np.zeros((128, 2048), np.float32)
    for p in range(128):
        e = p % 64
        pos = row if e < 32 else col
        a = (pos * inv[e % 16]).astype(np.float32)
        rc[p] = np.cos(a)
        rs[p] = np.sin(a)
    blk = np.zeros((128, 128), np.float32)
    blk[:64, :64] = 1.0
    blk[64:, 64:] = 1.0
    prm = np.zeros((128, 128), np.float32)
    for m in range(128):
        if (m % 32) < 16:
            prm[m + 16, m] = -1.0
        else:
            prm[m - 16, m] = 1.0
    return {
        "c_cs": cs.astype(np.float32).astype(bf), "c_csn": csn.astype(np.float32).astype(bf),
        "c_dc": dc.astype(np.float32).astype(bf), "c_ds": ds.astype(np.float32).astype(bf),
        "c_rc": rc, "c_rs": rs,
        "c_idb": np.eye(128, dtype=np.float32).astype(bf), "c_idf": np.eye(128, dtype=np.float32),
        "c_blk": blk.astype(bf), "c_prm": prm.astype(bf),
    }


_NC_CACHE = {}


def _in_maps(inp, nl=DEPTH):
    f = lambda a: np.ascontiguousarray(np.asarray(a, dtype=np.float32))
    xp_, xs_ = f(inp["x_prompt"]), f(inp["x_sample"])
    ck_, cv_ = f(inp["cache_k"]), f(inp["cache_v"])
    lam4 = np.ascontiguousarray(np.stack([f(inp["lam_q1"]), f(inp["lam_k1"]), f(inp["lam_q2"]), f(inp["lam_k2"])], axis=1))
    shared = {k: f(inp[k])[:nl] for k in ("w_in", "w_fproj", "w_aproj", "w_out", "w_mod", "b_mod", "g_norm", "g_q", "g_k", "g_sub")}
    shared["lam4"] = lam4[:nl]
    cst = [_consts(0), _consts(1)]
    maps = []
    for core in range(8):
        b, j = core // 2, core % 2
        m = dict(shared)
        m.update(cst[j])
        m["xp"] = np.ascontiguousarray(xp_[2 * core:2 * core + 2].reshape(512, D))
        m["xs"] = np.ascontiguousarray(xs_[b, 2048 * j:2048 * (j + 1)])
        m["cvec"] = np.ascontiguousarray(np.stack([f(inp["c_ctx"]), f(inp["c"])[b]], axis=0))
        m["ck"] = np.ascontiguousarray(ck_[b].reshape(DEPTH, 512, 1024)[:nl])
        m["cv"] = np.ascontiguousarray(cv_[b].reshape(DEPTH, 512, 1024)[:nl])
        maps.append(m)
    return maps


def kernel(**inputs):
    if "nc" not in _NC_CACHE:
        _NC_CACHE["nc"] = build_program()
    nc = _NC_CACHE["nc"]
    maps = _in_maps(inputs)
    res = run_bass_kernel_spmd(nc, maps, core_ids=list(range(8)))
    r = res.results
    y_p = np.concatenate([r[c]["yp"].reshape(2, 256, D) for c in range(8)], axis=0).astype(np.float32)
    y_s = np.stack([np.concatenate([r[2 * b]["ys"], r[2 * b + 1]["ys"]], axis=0) for b in range(4)], axis=0).astype(np.float32)
    nk_ = np.concatenate([r[c]["nk"] for c in range(8)], axis=0).reshape(16, DEPTH, 256, NH, 2, 64).astype(np.float32)
    nv_ = np.concatenate([r[c]["nv"] for c in range(8)], axis=0).reshape(16, DEPTH, 256, NH, 128).astype(np.float32)
    return (y_p, y_s, nk_, nv_)
```

```python
import math
import numpy as np
import ml_dtypes
import concourse.bass as bass
import concourse.mybir as mybir
from concourse.bass_utils import run_bass_kernel_spmd

F32 = mybir.dt.float32
BF16 = mybir.dt.bfloat16
AF = mybir.ActivationFunctionType
ALU = mybir.AluOpType
AX = mybir.AxisListType

DEPTH = 4
D = 2048
KC = 16
NH = 8
EPS = 1e-6
NTOK = 2560
NTB = 5
PAIRS = [[0, 1], [2, 3], [4, 5], [6, 7]]


class Res:
    __slots__ = ("name", "w", "wl", "war", "rc", "rd", "ov", "lo", "hi", "excl")

    def __init__(self, name, lo=None, hi=None):
        self.name = name
        self.w = None
        self.wl = []
        self.war = []
        self.rc = {}
        self.rd = []
        self.ov = []
        self.lo = lo
        self.hi = hi
        self.excl = False

    def readers(self):
        return list(self.rc.values()) + self.rd

    def writers(self):
        return ([self.w] if self.w is not None else []) + self.wl


class Op:
    __slots__ = ("eng", "fn", "dma", "deps", "sig", "sem", "val", "idx", "inc", "ring")

    def __init__(self, eng, fn, dma, idx, ring, inc):
        self.eng = eng
        self.fn = fn
        self.dma = dma
        self.deps = []
        self.sig = dma
        self.sem = None
        self.val = 0
        self.idx = idx
        self.inc = inc
        self.ring = ring


class Prog:
    ENGS = ("pe", "act", "dve", "pool", "sp")

    def __init__(self, nc):
        self.nc = nc
        self.ops = {e: [] for e in self.ENGS}
        self.n = 0

    def res(self, name):
        return Res(name)

    def add(self, eng, fn, reads=(), writes=(), pwrites=(), dma=False, cc=False):
        asyn = dma or cc
        op = Op(eng, fn, asyn, self.n, ("cc" if cc else eng), (1 if cc else 16) if asyn else 1)
        self.n += 1
        deps = {}

        def dep(o):
            if o is None:
                return
            if (not o.dma) and (not asyn) and o.eng == eng and eng == "pe":
                return
            if o.dma:
                deps[("d", o.idx)] = o
            else:
                k = ("c", o.eng)
                if k not in deps or deps[k].idx < o.idx:
                    deps[k] = o

        def dep_all(q):
            for o in q.writers():
                dep(o)
            for o in q.readers():
                dep(o)
            for o in q.war:
                dep(o)

        for r in reads:
            for q in [r] + r.ov:
                for o in q.writers():
                    dep(o)
            if r.excl:
                for o in r.rc.values():
                    if o.eng != eng:
                        dep(o)
        for w in writes:
            for q in [w] + w.ov:
                dep_all(q)
        for w in pwrites:
            rs = w.readers()
            if rs:
                w.war = rs + w.wl + ([w.w] if w.w is not None else [])
                w.rc = {}
                w.rd = []
                w.wl = []
                w.w = None
            for o in w.war:
                dep(o)
            if w.w is not None:
                dep(w.w)
            for q in w.ov:
                dep_all(q)
        op.deps = list(deps.values())
        for o in op.deps:
            o.sig = True
        for r in reads:
            if asyn:
                r.rd.append(op)
            else:
                r.rc[eng] = op
        for w in writes:
            w.w = op
            w.wl = []
            w.war = []
            w.rc = {}
            w.rd = []
        for w in pwrites:
            w.wl.append(op)
        self.ops[eng].append(op)
        return op

    def emit(self):
        nc = self.nc
        esem = {e: nc.alloc_semaphore("sem_" + e) for e in self.ENGS}
        rings = {"sp": [nc.alloc_semaphore(f"dsp{i}") for i in range(16)],
                 "pool": [nc.alloc_semaphore(f"dpl{i}") for i in range(12)],
                 "act": [nc.alloc_semaphore(f"dac{i}") for i in range(4)],
                 "cc": [nc.alloc_semaphore(f"dcc{i}") for i in range(8)]}
        rk = {k: 0 for k in rings}
        rcnt = {}
        final = {}
        last_on_sem = {}
        prev_of = {}
        for e in self.ENGS:
            cnt = 0
            for op in self.ops[e]:
                if op.dma:
                    ring = rings[op.ring]
                    s = ring[rk[op.ring] % len(ring)]
                    rk[op.ring] += 1
                    rcnt[s] = rcnt.get(s, 0) + op.inc
                    op.sem = s
                    op.val = rcnt[s]
                    final[s] = rcnt[s]
                    prev_of[op.idx] = last_on_sem.get(s)
                    last_on_sem[s] = op
                elif op.sig:
                    cnt += 1
                    op.sem = esem[e]
                    op.val = cnt
        prog = self

        def run(e, eh, tail=False):
            waited = {}

            def w(sem, val):
                if waited.get(sem, 0) < val:
                    eh.wait_ge(sem, val)
                    waited[sem] = val

            import os
            dbg = os.environ.get("KDBG_EMIT")
            for op in prog.ops[e]:
                if dbg:
                    print("EMIT", e, op.idx, "line", op.fn.__code__.co_firstlineno, "dma" if op.dma else "c",
                          "sig" if op.sig else "-", getattr(op.sem, "name", op.sem), op.val,
                          "deps", [(d.eng, d.idx, getattr(d.sem, "name", d.sem), d.val) for d in op.deps])
                for d in op.deps:
                    w(d.sem, d.val)
                if op.dma:
                    p = prev_of[op.idx]
                    if p is not None:
                        w(p.sem, p.val)
                ins = op.fn(eh)
                if op.sig:
                    ins.then_inc(op.sem, op.inc)
            if tail:
                for s_, v in final.items():
                    w(s_, v)

        with nc.allow_non_contiguous_dma(reason="small strided vector loads"):
            with nc.Block() as block:
                @block.sync
                def _(eh):
                    run("sp", eh, tail=True)

                @block.gpsimd
                def _(eh):
                    run("pool", eh)

                @block.vector
                def _(eh):
                    run("dve", eh)

                @block.scalar
                def _(eh):
                    run("act", eh)

                @block.tensor
                def _(eh):
                    run("pe", eh)


class Arena:
    def __init__(self, nc, prog, nbytes):
        self.t = nc.alloc_sbuf_tensor("arena", [128, nbytes // 2], BF16)
        self.nbytes = nbytes
        self.prog = prog
        self.all = []

    def multi(self, name, lo, hi, n):
        rs = [Res(f"{name}{i}", lo, hi) for i in range(n)]
        for q in self.all:
            if q.lo < hi and lo < q.hi:
                for r in rs:
                    q.ov.append(r)
                    r.ov.append(q)
        self.all.extend(rs)
        return rs

    def buf(self, name, off, shape, dt, res=True):
        esz = 2 if dt == BF16 else 4
        n = 1
        for s in shape:
            n *= s
        nb = n * esz
        assert off % 4 == 0 and off + nb <= self.nbytes, (name, off, nb, self.nbytes)
        ap = self.t[:, off // 2:(off + nb) // 2]
        if dt != BF16:
            ap = ap.bitcast(dt)
        if len(shape) == 2:
            ap = ap.rearrange("p (a b) -> p a b", a=shape[0])
        elif len(shape) == 3:
            ap = ap.rearrange("p (a b c) -> p a b c", a=shape[0], b=shape[1])
        if not res:
            return ap, None
        r = Res(name, off, off + nb)
        for q in self.all:
            if q.lo < r.hi and r.lo < q.hi:
                q.ov.append(r)
                r.ov.append(q)
        self.all.append(r)
        return ap, r


W_BLK = {"uf": (0, 2), "zf": (2, 4), "q": (4, 6), "k": (6, 8), "v": (8, 10),
         "za": (10, 12), "gf": (12, 16), "ga": (16, 20)}


def build_program(n_layers=DEPTH, upto="MPVABXCDE"):
    nc = bass.Bass("TRN2", target_bir_lowering=False)
    pg = Prog(nc)

    used_inputs = []
    nc._used_inputs = used_inputs

    class _Lazy:
        def __init__(self, name, shape, dt):
            self.a = (name, shape, dt)
            self.t = None

        def ap(self):
            if self.t is None:
                self.t = nc.dram_tensor(self.a[0], self.a[1], self.a[2], kind="ExternalInput")
                used_inputs.append(self.a[0])
            return self.t.ap()

    def din(name, shape, dt=F32):
        return _Lazy(name, shape, dt)

    def dout(name, shape, dt=F32):
        return nc.dram_tensor(name, shape, dt, kind="ExternalOutput")

    def dscr(name, shape, dt=BF16):
        return nc.dram_tensor(name, shape, dt)

    xp = din("xp", [512, D])
    xs = din("xs", [2048, D])
    cvec = din("cvec", [2, D])
    NL = n_layers
    ck = din("ck", [NL, 512, 1024])
    cvv = din("cv", [NL, 512, 1024])
    w_in = din("w_in", [NL, D, 10240])
    w_fproj = din("w_fproj", [NL, 1024, D])
    w_aproj = din("w_aproj", [NL, 1024, D])
    w_out = din("w_out", [NL, D, D])
    w_mod = din("w_mod", [NL, D, 3 * D])
    b_mod = din("b_mod", [NL, 3 * D])
    g_norm = din("g_norm", [NL, D])
    g_q = din("g_q", [NL, 64])
    g_k = din("g_k", [NL, 64])
    g_sub = din("g_sub", [NL, 128])
    lam4 = din("lam4", [NL, 4, 64])
    c_cs = din("c_cs", [256, 512], BF16)
    c_csn = din("c_csn", [256, 512], BF16)
    c_dc = din("c_dc", [4096, 2048], BF16)
    c_ds = din("c_ds", [4096, 2048], BF16)
    c_rc = din("c_rc", [128, 2048])
    c_rs = din("c_rs", [128, 2048])
    c_idb = din("c_idb", [128, 128], BF16)
    c_idf = din("c_idf", [128, 128])
    c_blk = din("c_blk", [128, 128], BF16)
    c_prm = din("c_prm", [128, 128], BF16)

    yp = dout("yp", [512, D])
    ys = dout("ys", [2048, D])
    nk = dout("nk", [2, NL, 256, 1024])
    nv = dout("nv", [2, NL, 256, 1024])

    hT_d = dscr("hT_d", [KC, 128, NTOK])
    modrows = dscr("modrows", [NL, 2, 3 * D], F32)
    kcT_d = dscr("kcT_d", [NL, NH, 128, 512])
    kpT_d = dscr("kpT_d", [NH, 128, 512])
    vp_d = dscr("vp_d", [512, 1024])
    abp_d = dscr("abp_d", [512, 2048])
    yfT_d = dscr("yfT_d", [8, 128, NTOK])
    yaT_d = dscr("yaT_d", [8, 128, NTOK])
    kx_in = [[dscr(f"kxi{l}_{c}", [512, 2048]) for c in range(2)] for l in range(n_layers)]
    kx_out = [[dscr(f"kxo{l}_{c}", [1024, 2048]) for c in range(2)] for l in range(n_layers)]
    vx_in = [[dscr(f"vxi{l}_{c}", [1024, 1024]) for c in range(2)] for l in range(n_layers)]
    vx_out = [[dscr(f"vxo{l}_{c}", [2048, 1024]) for c in range(2)] for l in range(n_layers)]
    ax_in = [[dscr(f"axi{l}_{c}", [512, 2048]) for c in range(4)] for l in range(n_layers)]
    ax_out = [[dscr(f"axo{l}_{c}", [1024, 2048]) for c in range(4)] for l in range(n_layers)]

    R = pg.res
    r_yp = [R(f"yp{t}") for t in range(4)]
    r_ys = [R(f"ys{t}") for t in range(16)]
    r_hTd = [R(f"hTd{tb}") for tb in range(NTB)]
    r_mod = [R(f"modrows{l}") for l in range(DEPTH)]
    r_kcT = R("kcT")
    r_kpT = R("kpT")
    r_vp = R("vp")
    r_abp = R("abp")
    r_yfd = [R(f"yfd{tb}") for tb in range(NTB)]
    r_yad = [R(f"yad{tb}") for tb in range(NTB)]
    r_kxi = [[R(f"kxi{l}{c}") for c in range(2)] for l in range(n_layers)]
    r_kxo = [[R(f"kxo{l}{c}") for c in range(2)] for l in range(n_layers)]
    r_vxi = [[R(f"vxi{l}{c}") for c in range(2)] for l in range(n_layers)]
    r_vxo = [[R(f"vxo{l}{c}") for c in range(2)] for l in range(n_layers)]
    r_axi = [[R(f"axi{l}{c}") for c in range(4)] for l in range(n_layers)]
    r_axo = [[R(f"axo{l}{c}") for c in range(4)] for l in range(n_layers)]
    r_nk = R("nk")
    r_nv = R("nv")

    psum = [nc.alloc_psum_tensor(f"ps{i}", [128, 512], F32) for i in range(8)]
    r_ps = [R(f"ps{i}") for i in range(8)]
    for _q in r_ps:
        _q.excl = True

    ar = Arena(nc, pg, 206 * 1024)
    off = [0]

    def pers(name, shape, dt):
        esz = 2 if dt == BF16 else 4
        n = esz
        for s in shape:
            n *= s
        n = (n + 63) // 64 * 64
        o = off[0]
        off[0] += n
        return ar.buf(name, o, shape, dt)

    identb, r_identb = pers("identb", [128], BF16)
    identf, r_identf = pers("identf", [128], F32)
    onesb, r_onesb = pers("onesb", [128], BF16)
    onesf, r_onesf = pers("onesf", [128], F32)
    blk64, r_blk64 = pers("blk64", [128], BF16)
    permb, r_permb = pers("permb", [128], BF16)
    cs256, r_cs256 = pers("cs256", [2, 512], BF16)
    csn256, r_csn256 = pers("csn256", [2, 512], BF16)
    epsc, r_epsc = pers("epsc", [1], F32)
    scol, r_scol = pers("scol", [KC, 2], BF16)
    craw, r_craw = pers("craw", [KC, 2], F32)
    geffT = [pers(f"geffT{c}", [KC], F32) for c in range(2)]
    shiftT = [pers(f"shiftT{c}", [KC], F32) for c in range(2)]
    scaleT = [pers(f"scaleT{c}", [KC], F32) for c in range(2)]
    gnT, r_gnT = pers("gnT", [KC], F32)
    gqc, r_gqc = pers("gqc", [1], F32)
    gkc, r_gkc = pers("gkc", [1], F32)
    gsc, r_gsc = pers("gsc", [1], F32)
    lamv, r_lamv = pers("lamv", [4, 64], F32)
    lamp, r_lamp = pers("lamp", [2, 64], F32)
    lams, r_lams = pers("lams", [4], F32)
    nlam, r_nlam = pers("nlam", [1], F32)
    ssq = [pers(f"ssq{i}", [4], F32) for i in range(2)]
    qT_off = off[0]
    qT_all, r_qT = pers("qT_all", [NH, NTOK], BF16)
    r_qTh = r_qT
    X0 = off[0]
    XSZ = 80 * 1024
    S0 = X0 + XSZ
    SSZ = 206 * 1024 - S0
    assert SSZ >= 72 * 1024, SSZ

    def cload(dst, rdst, src_ap):
        pg.add("sp", lambda e: e.dma_start(out=dst, in_=src_ap), writes=[rdst], dma=True)

    cload(identb, r_identb, c_idb.ap())
    cload(identf, r_identf, c_idf.ap())
    cload(blk64, r_blk64, c_blk.ap())
    cload(permb, r_permb, c_prm.ap())
    cload(cs256, r_cs256, c_cs.ap().rearrange("(k p) n -> p k n", p=128))
    cload(csn256, r_csn256, c_csn.ap().rearrange("(k p) n -> p k n", p=128))
    pg.add("dve", lambda e: e.memset(onesb, 1.0), writes=[r_onesb])
    pg.add("dve", lambda e: e.memset(onesf, 1.0), writes=[r_onesf])
    pg.add("dve", lambda e: e.memset(epsc, EPS), writes=[r_epsc])

    def mm_group(ps_i, pairs, extra_reads, n=512, m=128):
        def fn(e):
            ins = None
            k = len(pairs)
            for i, (a, b) in enumerate(pairs):
                ins = e.matmul(psum[ps_i][0:m, 0:n], a, b, start=(i == 0), stop=(i == k - 1))
            return ins
        return pg.add("pe", fn, reads=extra_reads, writes=[r_ps[ps_i]])

    bank_rr = {"main": [0, [0, 1, 2, 3]], "aux": [0, [4, 5]], "aux2": [0, [6, 7]]}

    def nbank(kind):
        st = bank_rr[kind]
        b = st[1][st[0] % len(st[1])]
        st[0] += 1
        return b

    class Slots:
        def __init__(self, name, base, n, shape, dt):
            esz = 2 if dt == BF16 else 4
            nb = esz
            for s in shape:
                nb *= s
            nb = (nb + 63) // 64 * 64
            self.items = [ar.buf(f"{name}{i}", base + i * nb, shape, dt) for i in range(n)]
            self.k = 0
            self.end = base + n * nb

        def next(self):
            it = self.items[self.k % len(self.items)]
            self.k += 1
            return it

    def m_scol():
        for cv_i in range(2):
            src = cvec.ap()[cv_i:cv_i + 1, :].rearrange("o (k p) -> p (o k)", p=128)
            pg.add("sp", lambda e, src=src, cv_i=cv_i: e.dma_start(out=craw[:, :, cv_i], in_=src),
                   pwrites=[r_craw], dma=True)
        pg.add("act", lambda e: e.activation(out=scol, in_=craw, func=AF.Silu), reads=[r_craw], writes=[r_scol])

    def make_M(l, base, bank_fn):
        o = base
        wsl = Slots(f"m_w{l}_", o, 2, [KC, 512], BF16); o = wsl.end
        brow, r_brow = ar.buf(f"m_brow{l}", o, [3 * D], F32); o += 3 * D * 4
        mrow = Slots(f"m_mrow{l}_", o, 2, [512], F32); o = mrow.end
        assert o <= S0 + SSZ, (o, S0 + SSZ)
        st = {}

        def load(cb):
            wt, r_wt = wsl.next()
            st[cb] = (wt, r_wt)
            src = w_mod.ap()[l, :, cb * 512:(cb + 1) * 512].rearrange("(k p) n -> p k n", p=128)
            pg.add("pool", lambda e: e.dma_start(out=wt, in_=src), writes=[r_wt], dma=True)

        def comp(cb):
            wt, r_wt = st[cb]
            b = bank_fn()
            mm_group(b, [(scol[:, kc, :], wt[:, kc, :]) for kc in range(KC)], [r_scol, r_wt], n=512, m=2)
            mr, r_mr = mrow.next()
            pg.add("dve", lambda e: e.tensor_tensor(
                out=mr[0:2, :], in0=psum[b][0:2, :], in1=brow[0:2, cb * 512:(cb + 1) * 512], op=ALU.add),
                reads=[r_ps[b], r_brow], writes=[r_mr])
            pg.add("sp", lambda e: e.dma_start(
                out=modrows.ap()[l, :, cb * 512:(cb + 1) * 512], in_=mr[0:2, :]),
                reads=[r_mr], pwrites=[r_mod[l]], dma=True)

        def first():
            for i in range(2):
                pg.add("sp", lambda e, i=i: e.dma_start(out=brow[i:i + 1, :], in_=b_mod.ap()[l:l + 1, :]),
                       pwrites=[r_brow], dma=True)
            load(0)
            load(1)

        items = [first]
        for cb in range(12):
            def it(cb=cb):
                comp(cb)
                if cb + 2 < 12:
                    load(cb + 2)
            items.append(it)
        return items

    def stage_M():
        m_scol()
        for it in make_M(0, S0, lambda: nbank("main")):
            it()

    def stage_P():
        o = S0
        ct = Slots("p_ct", o, 2, [1024], F32); o = ct.end
        cb16 = Slots("p_cb", o, 2, [1024], BF16); o = cb16.end
        co = Slots("p_co", o, 2, [NH, 128], BF16); o = co.end
        for l in range(n_layers):
            for t in range(4):
                c_t, r_ct = ct.next()
                pg.add("sp", lambda e, c_t=c_t, l=l, t=t: e.dma_start(out=c_t, in_=ck.ap()[l, t * 128:(t + 1) * 128, :]),
                       writes=[r_ct], dma=True)
                c_b, r_cb = cb16.next()
                pg.add("dve", lambda e, c_b=c_b, c_t=c_t: e.tensor_copy(out=c_b, in_=c_t), reads=[r_ct], writes=[r_cb])
                c_o, r_co = co.next()
                for hg in range(2):
                    b = nbank("main")

                    def fn(e, c_b=c_b, b=b, hg=hg):
                        ins = None
                        for hh in range(4):
                            h = hg * 4 + hh
                            ins = e.matmul(psum[b][:, hh * 128:(hh + 1) * 128], c_b[:, h * 128:(h + 1) * 128], identb,
                                           start=True, stop=True)
                        return ins
                    pg.add("pe", fn, reads=[r_cb, r_identb], writes=[r_ps[b]])
                    pg.add("act", lambda e, c_o=c_o, b=b, hg=hg: e.activation(
                        out=c_o[:, hg * 4:(hg + 1) * 4, :], in_=psum[b][:, :].rearrange("p (h n) -> p h n", h=4), func=AF.Copy),
                        reads=[r_ps[b]], pwrites=[r_co])
                pg.add("sp", lambda e, c_o=c_o, l=l, t=t: e.dma_start(
                    out=kcT_d.ap()[l, :, :, t * 128:(t + 1) * 128].rearrange("h p n -> p h n"), in_=c_o),
                    reads=[r_co], pwrites=[r_kcT], dma=True)

    def stage_V(l):
        lam_init = 0.8 - 0.6 * math.exp(-0.3 * l)
        pg.add("sp", lambda e: e.dma_start(out=gnT, in_=g_norm.ap()[l:l + 1, :].rearrange("o (k p) -> p (o k)", p=128)),
               writes=[r_gnT], dma=True)
        for cv_i in range(2):
            for (dst, rdst), lo in ((shiftT[cv_i], 0), (scaleT[cv_i], D)):
                src = modrows.ap()[l, cv_i:cv_i + 1, lo:lo + D].rearrange("o (k p) -> p (o k)", p=128)
                pg.add("sp", lambda e, dst=dst, src=src: e.dma_start(out=dst, in_=src),
                       reads=[r_mod[l]], writes=[rdst], dma=True)
            ge, r_ge = geffT[cv_i]
            sc, r_sc = scaleT[cv_i]
            pg.add("dve", lambda e, ge=ge, sc=sc: e.scalar_tensor_tensor(
                out=ge, in0=sc, scalar=1.0, in1=gnT, op0=ALU.add, op1=ALU.mult),
                reads=[r_sc, r_gnT], writes=[r_ge])
        for (dst, rdst, src_t, n) in ((gqc, r_gqc, g_q, 64), (gkc, r_gkc, g_k, 64)):
            for half in range(2):
                src = src_t.ap()[l:l + 1, :].rearrange("o n -> n o")
                pg.add("sp", lambda e, dst=dst, src=src, half=half: e.dma_start(out=dst[half * 64:(half + 1) * 64, :], in_=src),
                       pwrites=[rdst], dma=True)
        pg.add("sp", lambda e: e.dma_start(out=gsc, in_=g_sub.ap()[l:l + 1, :].rearrange("o n -> n o")),
               writes=[r_gsc], dma=True)
        pg.add("dve", lambda e: e.tensor_scalar(out=gsc, in0=gsc, scalar1=float(1.0 - lam_init), scalar2=None, op0=ALU.mult),
               reads=[r_gsc], writes=[r_gsc])
        pg.add("sp", lambda e: e.dma_start(out=lamv, in_=lam4.ap()[l:l + 1, :, :].broadcast_to([128, 4, 64])),
               writes=[r_lamv], dma=True)
        pg.add("dve", lambda e: e.tensor_tensor(out=lamp[:, 0, :], in0=lamv[:, 0, :], in1=lamv[:, 1, :], op=ALU.mult),
               reads=[r_lamv], pwrites=[r_lamp])
        pg.add("dve", lambda e: e.tensor_tensor(out=lamp[:, 1, :], in0=lamv[:, 2, :], in1=lamv[:, 3, :], op=ALU.mult),
               reads=[r_lamv], pwrites=[r_lamp])
        pg.add("dve", lambda e: e.tensor_reduce(out=lams[:, 0:2], in_=lamp, axis=AX.X, op=ALU.add),
               reads=[r_lamp], writes=[r_lams])
        pg.add("act", lambda e: e.activation(out=lams[:, 2:4], in_=lams[:, 0:2], func=AF.Exp),
               reads=[r_lams], writes=[r_lams])
        pg.add("dve", lambda e: e.scalar_tensor_tensor(
            out=nlam, in0=lams[:, 3:4], scalar=float(-lam_init), in1=lams[:, 2:3], op0=ALU.add, op1=ALU.subtract),
            reads=[r_lams], writes=[r_nlam])

    hT_all, _ = ar.buf("hT_all", X0, [KC, NTOK], BF16, res=False)
    _r = ar.multi("hTt", X0, X0 + KC * NTOK * 2, 40)
    r_hTt = [[_r[2 * t], _r[2 * t + 1]] for t in range(20)]

    def hT_tb(tb):
        out = []
        for i in range(4):
            out += r_hTt[tb * 4 + i]
        return out

    def x_src(l, t):
        if t < 4:
            t_, r_ = (xp if l == 0 else yp), r_yp[t]
            return t_.ap()[t * 128:(t + 1) * 128, :], r_
        t2 = t - 4
        t_, r_ = (xs if l == 0 else ys), r_ys[t2]
        return t_.ap()[t2 * 128:(t2 + 1) * 128, :], r_

    def stage_A(l):
        o = S0
        xt = Slots("a_xt", o, 2, [D], F32); o = xt.end
        xnb = Slots("a_xn", o, 2, [D], BF16); o = xnb.end
        junk, r_junk = ar.buf("a_junk", o, [D], BF16); o += D * 2
        for t in range(20):
            cv_i = 0 if t < 4 else 1
            x_t, r_xt = xt.next()
            src, r_src = x_src(l, t)
            pg.add("sp", lambda e, x_t=x_t, src=src: e.dma_start(out=x_t, in_=src),
                   reads=([r_src] if l > 0 else []), writes=[r_xt], dma=True)
            sq, r_sq = ssq[t % 2]
            pg.add("dve", lambda e, sq=sq: e.memset(sq, 0.0), writes=[r_sq])
            pg.add("act", lambda e, x_t=x_t, sq=sq: e.activation(out=junk, in_=x_t, func=AF.Square, accum_out=sq[:, 0:1]),
                   reads=[r_xt], writes=[r_junk, r_sq])
            pg.add("act", lambda e, sq=sq: e.activation(out=sq[:, 1:2], in_=sq[:, 0:1], func=AF.Ln, scale=1.0 / D, bias=epsc),
                   reads=[r_sq, r_epsc], writes=[r_sq])
            pg.add("act", lambda e, sq=sq: e.activation(out=sq[:, 2:3], in_=sq[:, 1:2], func=AF.Exp, scale=-0.5),
                   reads=[r_sq], writes=[r_sq])
            xn, r_xn = xnb.next()
            pg.add("dve", lambda e, xn=xn, x_t=x_t, sq=sq: e.tensor_scalar(
                out=xn, in0=x_t, scalar1=sq[:, 2:3], scalar2=None, op0=ALU.mult),
                reads=[r_xt, r_sq], writes=[r_xn])
            ge, r_ge = geffT[cv_i]
            sh, r_sh = shiftT[cv_i]
            for q4 in range(4):
                b = nbank("main")

                def fn(e, xn=xn, b=b, q4=q4):
                    ins = None
                    for j in range(4):
                        kc = q4 * 4 + j
                        ins = e.matmul(psum[b][:, j * 128:(j + 1) * 128], xn[:, kc * 128:(kc + 1) * 128], identb,
                                       start=True, stop=True)
                    return ins
                pg.add("pe", fn, reads=[r_xn, r_identb], writes=[r_ps[b]])
                for j in range(4):
                    kc = q4 * 4 + j
                    dst = hT_all[:, kc, t * 128:(t + 1) * 128]
                    srcp = psum[b][:, j * 128:(j + 1) * 128]
                    if q4 % 2 == 0:
                        pg.add("dve", lambda e, dst=dst, srcp=srcp, ge=ge, sh=sh, kc=kc: e.tensor_scalar(
                            out=dst, in0=srcp, scalar1=ge[:, kc:kc + 1], scalar2=sh[:, kc:kc + 1],
                            op0=ALU.mult, op1=ALU.add),
                            reads=[r_ps[b], r_ge, r_sh], pwrites=[r_hTt[t][0]])
                    else:
                        pg.add("act", lambda e, dst=dst, srcp=srcp, ge=ge, sh=sh, kc=kc: e.activation(
                            out=dst, in_=srcp, func=AF.Identity, scale=ge[:, kc:kc + 1], bias=sh[:, kc:kc + 1]),
                            reads=[r_ps[b], r_ge, r_sh], pwrites=[r_hTt[t][1]])
        for tb in range(NTB):
            pg.add("sp", lambda e, tb=tb: e.dma_start(
                out=hT_d.ap()[:, :, tb * 512:(tb + 1) * 512].rearrange("k p n -> p k n"),
                in_=hT_all[:, :, tb * 512:(tb + 1) * 512]),
                reads=hT_tb(tb), writes=[r_hTd[tb]], dma=True)

    def stage_B(l):
        o = S0
        wsl = Slots("b_w", o, 2, [KC, 512], BF16); o = wsl.end
        cosT, r_cos = ar.buf("b_cos", o, [2048], F32); o += 8192
        sinT, r_sin = ar.buf("b_sin", o, [2048], F32); o += 8192
        sqb = Slots("b_sq", o, 1, [512], BF16); o = sqb.end
        lnv = Slots("b_ln", o, 1, [512], F32); o = lnv.end
        qn = Slots("b_qn", o, 3, [512], F32); o = qn.end
        qnb = Slots("b_qnb", o, 2, [512], BF16); o = qnb.end
        t1s = Slots("b_t1", o, 1, [512], F32); o = t1s.end
        t2s = Slots("b_t2", o, 1, [512], F32); o = t2s.end
        outb = Slots("b_ob", o, 1, [512], BF16); o = outb.end
        uT = Slots("b_uT", o, 1, [2, 512], BF16); o = uT.end
        abst = Slots("b_ab", o, 1, [4, 512], BF16); o = abst.end
        nkst = Slots("b_nk", o, 1, [4, 128], F32); o = nkst.end
        vst = Slots("b_vs", o, 2, [512], BF16); o = vst.end
        vstf = Slots("b_vf", o, 1, [512], F32); o = vstf.end
        assert o <= S0 + SSZ, (o, S0 + SSZ)
        pg.add("sp", lambda e: e.dma_start(out=cosT, in_=c_rc.ap()), writes=[r_cos], dma=True)
        pg.add("sp", lambda e: e.dma_start(out=sinT, in_=c_rs.ap()), writes=[r_sin], dma=True)

        def load_w(blk):
            wt, r_wt = wsl.next()
            src = w_in.ap()[l, :, blk * 512:(blk + 1) * 512].rearrange("(k p) n -> p k n", p=128)
            pg.add("pool", lambda e: e.dma_start(out=wt, in_=src), writes=[r_wt], dma=True)
            return wt, r_wt

        import os
        _parts = os.environ.get("KDBG_B", "uf,q,k,v").split(",")
        for bi, blk in enumerate(range(*W_BLK["uf"]) if "uf" in _parts else []):
            wt, r_wt = load_w(blk)
            for gl in range(2):
                g = bi * 2 + gl
                for tb in range(NTB):
                    u_t, r_ut = uT.next()
                    for cc in range(2):
                        b = nbank("main")
                        c0 = gl * 256 + cc * 128
                        mm_group(b, [(wt[:, kc, c0:c0 + 128], hT_all[:, kc, tb * 512:(tb + 1) * 512]) for kc in range(KC)],
                                 [r_wt] + hT_tb(tb))
                        if cc == 0:
                            pg.add("act", lambda e, u_t=u_t, b=b: e.activation(out=u_t[:, 0, :], in_=psum[b][:, :], func=AF.Copy),
                                   reads=[r_ps[b]], pwrites=[r_ut])
                        else:
                            pg.add("dve", lambda e, u_t=u_t, b=b: e.tensor_copy(out=u_t[:, 1, :], in_=psum[b][:, :]),
                                   reads=[r_ps[b]], pwrites=[r_ut])
                    ab, r_ab = abst.next()
                    for tt in range(4):
                        b = nbank("aux")
                        mm_group(b, [(u_t[:, k2, tt * 128:(tt + 1) * 128], cs256[:, k2, :]) for k2 in range(2)],
                                 [r_ut, r_cs256])
                        if tt % 2 == 0:
                            pg.add("act", lambda e, ab=ab, b=b, tt=tt: e.activation(out=ab[:, tt, :], in_=psum[b][:, :], func=AF.Copy),
                                   reads=[r_ps[b]], pwrites=[r_ab])
                        else:
                            pg.add("dve", lambda e, ab=ab, b=b, tt=tt: e.tensor_copy(out=ab[:, tt, :], in_=psum[b][:, :]),
                                   reads=[r_ps[b]], pwrites=[r_ab])
                    if tb == 0:
                        dst = abp_d.ap()[:, g * 512:(g + 1) * 512].rearrange("(t p) n -> p t n", p=128)
                        rd = r_abp
                    else:
                        c = tb - 1
                        dst = ax_in[l][c].ap()[:, g * 512:(g + 1) * 512].rearrange("(t p) n -> p t n", p=128)
                        rd = r_axi[l][c]
                    pg.add("sp", lambda e, dst=dst, ab=ab: e.dma_start(out=dst, in_=ab), reads=[r_ab], pwrites=[rd], dma=True)

        def make_unit(which, h, tb, b):
            gcol, r_gcol = (gqc, r_gqc) if which == "q" else (gkc, r_gkc)
            st = {}

            def ph1():
                sq_t, r_sq = sqb.next()
                pg.add("act", lambda e: e.activation(out=sq_t, in_=psum[b][:, :], func=AF.Square),
                       reads=[r_ps[b]], writes=[r_sq])
                b2 = nbank("aux")
                mm_group(b2, [(blk64, sq_t)], [r_blk64, r_sq])
                ln_t, r_ln = lnv.next()
                pg.add("act", lambda e: e.activation(out=ln_t, in_=psum[b2][:, :], func=AF.Ln, scale=1.0 / 64, bias=epsc),
                       reads=[r_ps[b2], r_epsc], writes=[r_ln])
                pg.add("act", lambda e: e.activation(out=ln_t, in_=ln_t, func=AF.Exp, scale=-0.5),
                       reads=[r_ln], writes=[r_ln])
                qn_t, r_qn = qn.next()
                pg.add("dve", lambda e: e.scalar_tensor_tensor(
                    out=qn_t, in0=psum[b][:, :], scalar=gcol[:, 0:1], in1=ln_t, op0=ALU.mult, op1=ALU.mult),
                    reads=[r_ps[b], r_gcol, r_ln], writes=[r_qn])
                st["qn"] = (qn_t, r_qn)
                if tb > 0:
                    qb_t, r_qb = qnb.next()
                    pg.add("dve", lambda e: e.tensor_copy(out=qb_t, in_=qn_t), reads=[r_qn], writes=[r_qb])
                    st["qb"] = (qb_t, r_qb)

            def ph2():
                qn_t, r_qn = st["qn"]
                ob, r_ob = outb.next()
                if tb == 0:
                    pg.add("dve", lambda e: e.tensor_copy(out=ob, in_=qn_t), reads=[r_qn], writes=[r_ob])
                else:
                    qb_t, r_qb = st["qb"]
                    b3 = nbank("aux2")
                    mm_group(b3, [(permb, qb_t)], [r_permb, r_qb])
                    t1, r_t1 = t1s.next()
                    t2, r_t2 = t2s.next()
                    cs = (tb - 1) * 512
                    pg.add("dve", lambda e: e.tensor_tensor(out=t1, in0=qn_t, in1=cosT[:, cs:cs + 512], op=ALU.mult),
                           reads=[r_qn, r_cos], writes=[r_t1])
                    pg.add("dve", lambda e: e.tensor_tensor(out=t2, in0=psum[b3][:, :], in1=sinT[:, cs:cs + 512], op=ALU.mult),
                           reads=[r_ps[b3], r_sin], writes=[r_t2])
                    pg.add("dve", lambda e: e.tensor_tensor(out=ob, in0=t1, in1=t2, op=ALU.add),
                           reads=[r_t1, r_t2], writes=[r_ob])
                if which == "q":
                    pg.add("act", lambda e: e.activation(
                        out=qT_all[:, h, tb * 512:(tb + 1) * 512], in_=ob, func=AF.Copy),
                        reads=[r_ob], pwrites=[r_qT])
                elif tb == 0:
                    qf, r_qf = qn_t, r_qn
                    pg.add("sp", lambda e: e.dma_start(out=kpT_d.ap()[h, :, :], in_=ob),
                           reads=[r_ob], pwrites=[r_kpT], dma=True)
                    b4 = nbank("aux2")

                    def fn(e):
                        ins = None
                        for tt in range(4):
                            ins = e.matmul(psum[b4][:, tt * 128:(tt + 1) * 128], qf[:, tt * 128:(tt + 1) * 128], identf,
                                           start=True, stop=True)
                        return ins
                    pg.add("pe", fn, reads=[r_qf, r_identf], writes=[r_ps[b4]])
                    ns, r_ns = nkst.next()
                    pg.add("act", lambda e: e.activation(
                        out=ns, in_=psum[b4][:, :].rearrange("p (t n) -> p t n", t=4), func=AF.Copy),
                        reads=[r_ps[b4]], writes=[r_ns])
                    for s_ in range(2):
                        pg.add("sp", lambda e, s_=s_: e.dma_start(
                            out=nk.ap()[s_, l, :, h * 128:(h + 1) * 128].rearrange("(t p) n -> p t n", p=128),
                            in_=ns[:, s_ * 2:(s_ + 1) * 2, :]),
                            reads=[r_ns], dma=True)
                else:
                    c = h // 4
                    pg.add("sp", lambda e: e.dma_start(
                        out=kx_in[l][c].ap()[(h % 4) * 128:(h % 4 + 1) * 128, (tb - 1) * 512:tb * 512], in_=ob),
                        reads=[r_ob], pwrites=[r_kxi[l][c]], dma=True)
            return ph1, ph2

        inflight = []

        def advance():
            if len(inflight) >= 2 and inflight[-2][0] is not None:
                inflight[-2][0]()
                inflight[-2][0] = None
            if len(inflight) >= 3:
                u = inflight.pop(0)
                u[1]()

        for which in ("q", "k"):
            for bi, blk in enumerate(range(*W_BLK[which]) if which in _parts else []):
                wt, r_wt = load_w(blk)
                for hl in range(4):
                    h = bi * 4 + hl
                    for tb in range(NTB):
                        b = nbank("main")
                        mm_group(b, [(wt[:, kc, hl * 128:(hl + 1) * 128], hT_all[:, kc, tb * 512:(tb + 1) * 512])
                                     for kc in range(KC)], [r_wt] + hT_tb(tb))
                        inflight.append(list(make_unit(which, h, tb, b)))
                        advance()
        while inflight:
            u = inflight.pop(0)
            if u[0] is not None:
                u[0]()
            u[1]()

        for bi, blk in enumerate(range(*W_BLK["v"]) if "v" in _parts else []):
            wt, r_wt = load_w(blk)
            for t in range(20):
                b = nbank("main")
                mm_group(b, [(hT_all[:, kc, t * 128:(t + 1) * 128], wt[:, kc, :]) for kc in range(KC)], [r_wt] + r_hTt[t])
                vs, r_vs = vst.next()
                pg.add("act", lambda e, vs=vs, b=b: e.activation(out=vs, in_=psum[b][:, :], func=AF.Copy),
                       reads=[r_ps[b]], writes=[r_vs])
                if t < 4:
                    pg.add("sp", lambda e, vs=vs, t=t, bi=bi: e.dma_start(
                        out=vp_d.ap()[t * 128:(t + 1) * 128, bi * 512:(bi + 1) * 512], in_=vs),
                        reads=[r_vs], pwrites=[r_vp], dma=True)
                    vf, r_vf = vstf.next()
                    pg.add("dve", lambda e, vf=vf, b=b: e.tensor_copy(out=vf, in_=psum[b][:, :]),
                           reads=[r_ps[b]], writes=[r_vf])
                    s, tt = t // 2, t % 2
                    pg.add("sp", lambda e, vf=vf, s=s, tt=tt, bi=bi: e.dma_start(
                        out=nv.ap()[s, l, tt * 128:(tt + 1) * 128, bi * 512:(bi + 1) * 512], in_=vf),
                        reads=[r_vf], dma=True)
                else:
                    t2 = t - 4
                    c, tl = t2 // 8, t2 % 8
                    pg.add("sp", lambda e, vs=vs, c=c, tl=tl, bi=bi: e.dma_start(
                        out=vx_in[l][c].ap()[tl * 128:(tl + 1) * 128, bi * 512:(bi + 1) * 512], in_=vs),
                        reads=[r_vs], pwrites=[r_vxi[l][c]], dma=True)

    def stage_X(l):
        def ag(src, dst, r_src, r_dst):
            pg.add("pool", lambda e: e.collective_compute(
                "AllGather", ALU.bypass, replica_groups=PAIRS, ins=[src.ap()], outs=[dst.ap()]),
                reads=[r_src], writes=[r_dst], cc=True)
        for c in range(2):
            ag(kx_in[l][c], kx_out[l][c], r_kxi[l][c], r_kxo[l][c])
        for c in range(2):
            ag(vx_in[l][c], vx_out[l][c], r_vxi[l][c], r_vxo[l][c])
        for c in range(4):
            ag(ax_in[l][c], ax_out[l][c], r_axi[l][c], r_axo[l][c])

    def stage_C(l):
        o = X0
        kTs = Slots("c_kT", o, 2, [4608], BF16); o = kTs.end
        vSs = Slots("c_v", o, 2, [36, 128], BF16); o = vSs.end
        pS = Slots("c_p", o, 5, [512], BF16); o = pS.end
        rc0 = Slots("c_r", o, 2, [512], F32); o = rc0.end
        tS = Slots("c_t", o, 2, [512], F32); o = tS.end
        aS = Slots("c_a", o, 2, [512], F32); o = aS.end
        sqS = Slots("c_sq", o, 2, [512], BF16); o = sqS.end
        lnS = Slots("c_ln", o, 2, [512], F32); o = lnS.end
        _acc = Slots("c_acc", o, 8, [512], F32); o = _acc.end
        smS = Slots("c_sm", o, 2, [512], BF16); o = smS.end
        accS = [[[_acc.items[par * 4 + c * 2 + k] for k in range(2)] for c in range(2)] for par in range(2)]
        obS = Slots("c_ob", o, 2, [512], BF16); o = obS.end
        assert o <= X0 + XSZ, (o, X0 + XSZ)
        o = S0
        kpa, r_kpa = ar.buf("c_kpa", o, [NH, 512], BF16); o += NH * 512 * 2
        vpa, r_vpa = ar.buf("c_vpa", o, [4, 1024], BF16); o += 4 * 1024 * 2

        sbank = [0]
        qblk = [0]
        pending = []

        def run_pending(n=1):
            for _ in range(n):
                if pending:
                    pending.pop(0)()

        def core(h, q_ap, nq, ktiles, kv_reads, out_dst, r_out):
            nk_ = len(ktiles)
            par = qblk[0] % 2
            qblk[0] += 1
            bo = (4, 5) if par == 0 else (6, 7)
            accs = [[accS[par][c][0], accS[par][c][1]] for c in range(2)]

            def s_step(i):
                kT, _ = ktiles[i]
                pr = sbank[0] % 2
                sbank[0] += 1
                b0, b1 = 2 * pr, 2 * pr + 1

                def fn(e):
                    e.matmul(psum[b0][:, 0:nq], kT[0:64, :], q_ap[0:64, :], start=True, stop=True)
                    return e.matmul(psum[b1][:, 0:nq], kT[64:128, :], q_ap[64:128, :], start=True, stop=True)
                pg.add("pe", fn, reads=kv_reads + [r_qT], writes=[r_ps[b0], r_ps[b1]])
                ps_ = []
                for c, bb in ((0, b0), (1, b1)):
                    p_t, r_p = pS.next()
                    pg.add("act", lambda e, p_t=p_t, bb=bb: e.activation(
                        out=p_t[:, 0:nq], in_=psum[bb][:, 0:nq], func=AF.Exp, scale=0.125),
                        reads=[r_ps[bb]], writes=[r_p])
                    ps_.append((p_t, r_p))
                    a_ap, r_acc = accs[c][i % 2]
                    if i < 2:
                        pg.add("dve", lambda e, a_ap=a_ap, p_t=p_t: e.tensor_copy(out=a_ap[:, 0:nq], in_=p_t[:, 0:nq]),
                               reads=[r_p], writes=[r_acc])
                    else:
                        pg.add("dve", lambda e, a_ap=a_ap, p_t=p_t: e.tensor_tensor(
                            out=a_ap[:, 0:nq], in0=a_ap[:, 0:nq], in1=p_t[:, 0:nq], op=ALU.add),
                            reads=[r_p, r_acc], writes=[r_acc])
                return ps_

            def o_step(i, ps_):
                _, v = ktiles[i]
                (p0, r_p0), (p1, r_p1) = ps_

                def fn(e):
                    st, sp_ = (i == 0), (i == nk_ - 1)
                    e.matmul(psum[bo[0]][:, 0:nq], v, p0[:, 0:nq], start=st, stop=sp_)
                    return e.matmul(psum[bo[1]][:, 0:nq], v, p1[:, 0:nq], start=st, stop=sp_)
                pg.add("pe", fn, reads=kv_reads + [r_p0, r_p1], writes=[r_ps[bo[0]], r_ps[bo[1]]])

            prev = s_step(0)
            for i in range(nk_):
                nxt = s_step(i + 1) if i + 1 < nk_ else None
                o_step(i, prev)
                prev = nxt
                if i == 1 or i == 3:
                    run_pending()

            st_ = {}

            def epi1():
                pr = sbank[0] % 2
                sbank[0] += 1
                bl = (2 * pr, 2 * pr + 1)
                ts_ = []
                for c in range(2):
                    (a0, r_a0), (a1, r_a1) = accs[c]
                    sm, r_sm = smS.next()
                    pg.add("dve", lambda e, sm=sm, a0=a0, a1=a1: e.tensor_tensor(
                        out=sm[:, 0:nq], in0=a0[:, 0:nq], in1=a1[:, 0:nq], op=ALU.add),
                        reads=[r_a0, r_a1], writes=[r_sm])
                    mm_group(bl[c], [(onesb, sm[:, 0:nq])], [r_onesb, r_sm], n=nq)
                    r_t, r_r = rc0.next()
                    pg.add("dve", lambda e, r_t=r_t, c=c: e.reciprocal(out=r_t[:, 0:nq], in_=psum[bl[c]][:, 0:nq]),
                           reads=[r_ps[bl[c]]], writes=[r_r])
                    t_t, r_tt = tS.next()
                    pg.add("dve", lambda e, t_t=t_t, r_t=r_t, c=c: e.tensor_tensor(
                        out=t_t[:, 0:nq], in0=psum[bo[c]][:, 0:nq], in1=r_t[:, 0:nq], op=ALU.mult),
                        reads=[r_ps[bo[c]], r_r], writes=[r_tt])
                    ts_.append((t_t, r_tt))
                a_t, r_a = aS.next()
                (t0, r_t0), (t1, r_t1) = ts_
                pg.add("dve", lambda e: e.scalar_tensor_tensor(
                    out=a_t[:, 0:nq], in0=t1[:, 0:nq], scalar=nlam[:, 0:1], in1=t0[:, 0:nq], op0=ALU.mult, op1=ALU.add),
                    reads=[r_t0, r_t1, r_nlam], writes=[r_a])
                sq_t, r_sq = sqS.next()
                pg.add("act", lambda e: e.activation(out=sq_t[:, 0:nq], in_=a_t[:, 0:nq], func=AF.Square),
                       reads=[r_a], writes=[r_sq])
                st_["a"] = (a_t, r_a, sq_t, r_sq)

            def epi2():
                a_t, r_a, sq_t, r_sq = st_["a"]
                pr = sbank[0] % 2
                sbank[0] += 1
                bq = 2 * pr
                mm_group(bq, [(onesb, sq_t[:, 0:nq])], [r_onesb, r_sq], n=nq)
                ln_t, r_ln = lnS.next()
                pg.add("act", lambda e: e.activation(out=ln_t[:, 0:nq], in_=psum[bq][:, 0:nq], func=AF.Ln, scale=1.0 / 128, bias=epsc),
                       reads=[r_ps[bq], r_epsc], writes=[r_ln])
                pg.add("act", lambda e: e.activation(out=ln_t[:, 0:nq], in_=ln_t[:, 0:nq], func=AF.Exp, scale=-0.5),
                       reads=[r_ln], writes=[r_ln])
                ob, r_ob = obS.next()
                pg.add("dve", lambda e: e.scalar_tensor_tensor(
                    out=ob[:, 0:nq], in0=a_t[:, 0:nq], scalar=gsc[:, 0:1], in1=ln_t[:, 0:nq], op0=ALU.mult, op1=ALU.mult),
                    reads=[r_a, r_gsc, r_ln], writes=[r_ob])
                pg.add("sp", lambda e: e.dma_start(out=out_dst, in_=ob[:, 0:nq]), reads=[r_ob], pwrites=[r_out], dma=True)

            run_pending(len(pending))
            pending.append(epi1)
            pending.append(epi2)

        pg.add("sp", lambda e: e.dma_start(out=kpa, in_=kpT_d.ap().rearrange("h p n -> p h n")),
               reads=[r_kpT], writes=[r_kpa], dma=True)
        pg.add("sp", lambda e: e.dma_start(out=vpa, in_=vp_d.ap().rearrange("(t p) n -> p t n", p=128)),
               reads=[r_vp], writes=[r_vpa], dma=True)
        for s in range(2):
            for h in range(NH):
                kt = [(kpa[:, h, (s * 2 + i) * 128:(s * 2 + i + 1) * 128], vpa[:, s * 2 + i, h * 128:(h + 1) * 128])
                      for i in range(2)]
                core(h, qT_all[:, h, s * 256:(s + 1) * 256], 256, kt, [r_kpa, r_vpa],
                     yaT_d.ap()[h, :, s * 256:(s + 1) * 256], r_yad[0])

        def load_kv(h):
            kT, r_kT = kTs.next()
            vS, r_vS = vSs.next()
            pg.add("sp", lambda e: e.dma_start(out=kT[:, 0:512], in_=kcT_d.ap()[l, h, :, :]),
                   reads=[r_kcT], pwrites=[r_kT], dma=True)
            c, hl = h // 4, h % 4
            for r in range(2):
                pg.add("sp", lambda e, r=r: e.dma_start(
                    out=kT[:, 512 + r * 2048:512 + (r + 1) * 2048],
                    in_=kx_out[l][c].ap()[r * 512 + hl * 128:r * 512 + (hl + 1) * 128, :]),
                    reads=[r_kxo[l][c]], pwrites=[r_kT], dma=True)
            pg.add("pool", lambda e: e.dma_start(
                out=vS[:, 0:4, :], in_=cvv.ap()[l, :, h * 128:(h + 1) * 128].rearrange("(t p) n -> p t n", p=128)),
                pwrites=[r_vS], dma=True)
            for r in range(2):
                for c2 in range(2):
                    t0 = 4 + r * 16 + c2 * 8
                    pg.add("sp", lambda e, r=r, c2=c2, t0=t0: e.dma_start(
                        out=vS[:, t0:t0 + 8, :],
                        in_=vx_out[l][c2].ap()[r * 1024:(r + 1) * 1024, h * 128:(h + 1) * 128].rearrange("(t p) n -> p t n", p=128)),
                        reads=[r_vxo[l][c2]], pwrites=[r_vS], dma=True)
            return kT, r_kT, vS, r_vS

        m_items = []
        if l + 1 < n_layers:
            def _bank():
                pr = sbank[0] % 2
                sbank[0] += 1
                return 2 * pr
            m_items = make_M(l + 1, S0 + 16 * 1024, _bank)
            m_items.pop(0)()
        qcount = [0]
        nxt = load_kv(0)
        for h in range(NH):
            kT, r_kT, vS, r_vS = nxt
            if h + 1 < NH:
                nxt = load_kv(h + 1)
            kt = [(kT[:, i * 128:(i + 1) * 128], vS[:, i, :]) for i in range(36)]
            for qb in range(4):
                tb = 1 + qb
                core(h, qT_all[:, h, tb * 512:(tb + 1) * 512], 512, kt, [r_kT, r_vS],
                     yaT_d.ap()[h, :, tb * 512:(tb + 1) * 512], r_yad[tb])
                qcount[0] += 1
                if m_items and qcount[0] % 2 == 0:
                    m_items.pop(0)()
        run_pending(len(pending))
        while m_items:
            m_items.pop(0)()

    def stage_D(l):
        o = X0
        dC, r_dC = ar.buf("d_C", o, [32, 512], BF16); o += 32 * 512 * 2
        dS, r_dS = ar.buf("d_S", o, [32, 512], BF16); o += 32 * 512 * 2
        assert o <= X0 + XSZ
        o = S0
        abS = Slots("d_ab", o, 2, [2, 32, 128], BF16); o = abS.end
        abp, r_abpS = ar.buf("d_abp", o, [4, 2048], BF16); o += 4 * 2048 * 2
        yo = Slots("d_yo", o, 2, [512], BF16); o = yo.end
        assert o <= S0 + SSZ
        pg.add("sp", lambda e: e.dma_start(out=abp, in_=abp_d.ap().rearrange("(t p) n -> p t n", p=128)),
               reads=[r_abp], writes=[r_abpS], dma=True)
        for mc in range(8):
            g, cc = mc // 2, mc % 2
            b = nbank("main")
            for s in range(2):
                def fn(e, s=s, b=b, g=g, cc=cc):
                    ins = None
                    for i in range(2):
                        t = s * 2 + i
                        a_ap = abp[:, t, g * 512 + cc * 128:g * 512 + cc * 128 + 128]
                        b_ap = abp[:, t, g * 512 + 256 + cc * 128:g * 512 + 256 + cc * 128 + 128]
                        e.matmul(psum[b][:, s * 256:(s + 1) * 256], a_ap, csn256[:, i, 0:256], start=(i == 0), stop=False)
                        ins = e.matmul(psum[b][:, s * 256:(s + 1) * 256], b_ap, csn256[:, i, 256:512], start=False, stop=(i == 1))
                    return ins
                pg.add("pe", fn, reads=[r_abpS, r_csn256], writes=[r_ps[b]])
            y_t, r_y = yo.next()
            pg.add("act", lambda e, y_t=y_t, b=b: e.activation(out=y_t, in_=psum[b][:, :], func=AF.Copy),
                   reads=[r_ps[b]], writes=[r_y])
            pg.add("sp", lambda e, y_t=y_t, mc=mc: e.dma_start(out=yfT_d.ap()[mc, :, 0:512], in_=y_t),
                   reads=[r_y], pwrites=[r_yfd[0]], dma=True)
        for kb in range(4):
            pg.add("sp", lambda e, kb=kb: e.dma_start(
                out=dC, in_=c_dc.ap()[:, kb * 512:(kb + 1) * 512].rearrange("(t p) n -> p t n", p=128)),
                writes=[r_dC], dma=True)
            pg.add("sp", lambda e, kb=kb: e.dma_start(
                out=dS, in_=c_ds.ap()[:, kb * 512:(kb + 1) * 512].rearrange("(t p) n -> p t n", p=128)),
                writes=[r_dS], dma=True)
            for mc in range(8):
                g, cc = mc // 2, mc % 2
                ab, r_ab = abS.next()
                for c in range(4):
                    for r in range(2):
                        t0 = r * 16 + c * 4
                        for pl in range(2):
                            col = g * 512 + pl * 256 + cc * 128
                            pg.add("sp", lambda e, ab=ab, c=c, r=r, t0=t0, pl=pl, col=col: e.dma_start(
                                out=ab[:, pl, t0:t0 + 4, :],
                                in_=ax_out[l][c].ap()[r * 512:(r + 1) * 512, col:col + 128].rearrange("(t p) n -> p t n", p=128)),
                                reads=[r_axo[l][c]], pwrites=[r_ab], dma=True)
                b = nbank("main")
                pairs = []
                for t in range(32):
                    pairs.append((ab[:, 0, t, :], dC[:, t, :]))
                    pairs.append((ab[:, 1, t, :], dS[:, t, :]))
                mm_group(b, pairs, [r_ab, r_dC, r_dS])
                y_t, r_y = yo.next()
                pg.add("act", lambda e, y_t=y_t, b=b: e.activation(out=y_t, in_=psum[b][:, :], func=AF.Copy),
                       reads=[r_ps[b]], writes=[r_y])
                tb = 1 + kb
                pg.add("sp", lambda e, y_t=y_t, mc=mc, tb=tb: e.dma_start(
                    out=yfT_d.ap()[mc, :, tb * 512:(tb + 1) * 512], in_=y_t),
                    reads=[r_y], pwrites=[r_yfd[tb]], dma=True)

    def stage_E(l, last):
        o = X0
        hS, r_hS = ar.buf("e_h", o, [KC, 1024], BF16); o += KC * 1024 * 2
        mg, r_mg = ar.buf("e_mg", o, [KC, 1024], BF16); o += KC * 1024 * 2
        yfS, r_yfS = ar.buf("e_yf", o, [8, 1024], BF16); o += 8 * 1024 * 2
        assert o <= X0 + XSZ
        o = qT_off
        yaS, r_yaS = ar.buf("e_ya", o, [8, 1024], BF16); o += 8 * 1024 * 2
        gbc = [ar.buf(f"e_gate{c}", o + c * 8192, [D], F32) for c in range(2)]; o += 16384
        sgS = Slots("e_sg", o, 2, [512], F32); o = sgS.end
        assert o <= X0
        o = S0
        wsl = Slots("e_w", o, 2, [KC, 512], BF16); o = wsl.end
        wpj = Slots("e_wp", o, 2, [8, 512], BF16); o = wpj.end
        xq = Slots("e_xq", o, 4, [512], F32); o = xq.end
        xo = Slots("e_xo", o, 4, [512], F32); o = xo.end
        tmp = Slots("e_tmp", o, 2, [512], F32); o = tmp.end
        assert o <= S0 + SSZ, (o, S0 + SSZ)
        for cv_i in range(2):
            g_ap, r_g = gbc[cv_i]
            pg.add("sp", lambda e, g_ap=g_ap, cv_i=cv_i: e.dma_start(
                out=g_ap, in_=modrows.ap()[l, cv_i:cv_i + 1, 2 * D:3 * D].broadcast_to([128, D])),
                reads=[r_mod[l]], writes=[r_g], dma=True)

        def load_w(blk):
            wt, r_wt = wsl.next()
            src = w_in.ap()[l, :, blk * 512:(blk + 1) * 512].rearrange("(k p) n -> p k n", p=128)
            pg.add("pool", lambda e: e.dma_start(out=wt, in_=src), writes=[r_wt], dma=True)
            return wt, r_wt

        def load_wp(wten, fb):
            wt, r_wt = wpj.next()
            src = wten.ap()[l, :, fb * 512:(fb + 1) * 512].rearrange("(k p) n -> p k n", p=128)
            pg.add("pool", lambda e: e.dma_start(out=wt, in_=src), writes=[r_wt], dma=True)
            return wt, r_wt

        for sb in ([0, 1], [2, 3], [4]):
            t0 = sb[0] * 512
            nt = len(sb) * 512
            pg.add("sp", lambda e, t0=t0, nt=nt: e.dma_start(
                out=hS[:, :, 0:nt], in_=hT_d.ap()[:, :, t0:t0 + nt].rearrange("k p n -> p k n")),
                reads=[r_hTd[tb] for tb in sb], writes=[r_hS], dma=True)
            pg.add("sp", lambda e, t0=t0, nt=nt: e.dma_start(
                out=yfS[:, :, 0:nt], in_=yfT_d.ap()[:, :, t0:t0 + nt].rearrange("k p n -> p k n")),
                reads=[r_yfd[tb] for tb in sb], writes=[r_yfS], dma=True)
            pg.add("sp", lambda e, t0=t0, nt=nt: e.dma_start(
                out=yaS[:, :, 0:nt], in_=yaT_d.ap()[:, :, t0:t0 + nt].rearrange("k p n -> p k n")),
                reads=[r_yad[tb] for tb in sb], writes=[r_yaS], dma=True)
            for which, ybuf, r_yb in (("zf", yfS, r_yfS), ("za", yaS, r_yaS)):
                for bi, blk in enumerate(range(*W_BLK[which])):
                    wt, r_wt = load_w(blk)
                    for cc in range(4):
                        mc = bi * 4 + cc
                        for ti in range(len(sb)):
                            b = nbank("main")
                            mm_group(b, [(wt[:, kc, cc * 128:(cc + 1) * 128], hS[:, kc, ti * 512:(ti + 1) * 512])
                                         for kc in range(KC)], [r_wt, r_hS])
                            sg, r_sg = sgS.next()
                            pg.add("act", lambda e, sg=sg, b=b: e.activation(out=sg, in_=psum[b][:, :], func=AF.Silu),
                                   reads=[r_ps[b]], writes=[r_sg])
                            dst = ybuf[:, mc, ti * 512:(ti + 1) * 512]
                            pg.add("dve", lambda e, dst=dst, sg=sg: e.tensor_tensor(out=dst, in0=dst, in1=sg, op=ALU.mult),
                                   reads=[r_sg, r_yb], writes=[r_yb])
            for pi, (which, wten, ybuf, r_yb) in enumerate((("gf", w_fproj, yfS, r_yfS), ("ga", w_aproj, yaS, r_yaS))):
                for bi, blk in enumerate(range(*W_BLK[which])):
                    wt, r_wt = load_w(blk)
                    wp, r_wp = load_wp(wten, bi)
                    for cc in range(4):
                        fo = bi * 4 + cc
                        for ti in range(len(sb)):
                            b = nbank("main")
                            mm_group(b, [(wt[:, kc, cc * 128:(cc + 1) * 128], hS[:, kc, ti * 512:(ti + 1) * 512])
                                         for kc in range(KC)], [r_wt, r_hS])
                            sg, r_sg = sgS.next()
                            pg.add("act", lambda e, sg=sg, b=b: e.activation(out=sg, in_=psum[b][:, :], func=AF.Sigmoid),
                                   reads=[r_ps[b]], writes=[r_sg])
                            b2 = nbank("aux")
                            mm_group(b2, [(wp[:, j, cc * 128:(cc + 1) * 128], ybuf[:, j, ti * 512:(ti + 1) * 512])
                                          for j in range(8)], [r_wp, r_yb])
                            dst = mg[:, fo, ti * 512:(ti + 1) * 512]
                            if pi == 0:
                                pg.add("dve", lambda e, dst=dst, sg=sg, b2=b2: e.tensor_tensor(
                                    out=dst, in0=psum[b2][:, :], in1=sg, op=ALU.mult),
                                    reads=[r_ps[b2], r_sg], writes=[r_mg])
                            else:
                                tm, r_tm = tmp.next()
                                pg.add("dve", lambda e, tm=tm, sg=sg, b2=b2: e.tensor_tensor(
                                    out=tm, in0=psum[b2][:, :], in1=sg, op=ALU.mult),
                                    reads=[r_ps[b2], r_sg], writes=[r_tm])
                                pg.add("dve", lambda e, dst=dst, tm=tm: e.tensor_tensor(out=dst, in0=dst, in1=tm, op=ALU.add),
                                       reads=[r_tm, r_mg], writes=[r_mg])
            for fb in range(4):
                wt, r_wt = wsl.next()
                src = w_out.ap()[l, :, fb * 512:(fb + 1) * 512].rearrange("(k p) n -> p k n", p=128)
                pg.add("pool", lambda e, wt=wt, src=src: e.dma_start(out=wt, in_=src), writes=[r_wt], dma=True)
                for tl in range(nt // 128):
                    t = sb[0] * 4 + tl
                    cv_i = 0 if t < 4 else 1
                    xsrc, r_xsrc = x_src(l, t)
                    x_q, r_xq = xq.next()
                    pg.add("sp", lambda e, x_q=x_q, xsrc=xsrc, fb=fb: e.dma_start(out=x_q, in_=xsrc[:, fb * 512:(fb + 1) * 512]),
                           reads=([r_xsrc] if l > 0 else []), writes=[r_xq], dma=True)
                    b = nbank("main")
                    mm_group(b, [(mg[:, j, tl * 128:(tl + 1) * 128], wt[:, j, :]) for j in range(KC)], [r_mg, r_wt])
                    tm, r_tm = tmp.next()
                    g_ap, r_g = gbc[cv_i]
                    pg.add("dve", lambda e, tm=tm, b=b, g_ap=g_ap, fb=fb: e.tensor_tensor(
                        out=tm, in0=psum[b][:, :], in1=g_ap[:, fb * 512:(fb + 1) * 512], op=ALU.mult),
                        reads=[r_ps[b], r_g], writes=[r_tm])
                    x_o, r_xo = xo.next()
                    pg.add("dve", lambda e, x_o=x_o, tm=tm, x_q=x_q: e.tensor_tensor(out=x_o, in0=tm, in1=x_q, op=ALU.add),
                           reads=[r_tm, r_xq], writes=[r_xo])
                    if t < 4:
                        dst, r_dst = yp.ap()[t * 128:(t + 1) * 128, fb * 512:(fb + 1) * 512], r_yp[t]
                    else:
                        dst, r_dst = ys.ap()[(t - 4) * 128:(t - 3) * 128, fb * 512:(fb + 1) * 512], r_ys[t - 4]
                    pg.add("sp", lambda e, dst=dst, x_o=x_o: e.dma_start(out=dst, in_=x_o),
                           reads=[r_xo], pwrites=[r_dst], dma=True)

    stages = upto
    if "M" in stages:
        stage_M()
    if "P" in stages:
        stage_P()
    for l in range(n_layers):
        if "V" in stages:
            stage_V(l)
        if "A" in stages:
            stage_A(l)
        if "B" in stages:
            stage_B(l)
        if "X" in stages:
            stage_X(l)
        if "C" in stages:
            stage_C(l)
        if "D" in stages:
            stage_D(l)
        if "E" in stages:
            stage_E(l, l == n_layers - 1)
    pg.emit()
    return nc


def _consts(j):
    bf = ml_dtypes.bfloat16
    c = np.arange(256, dtype=np.float64)
    ang = 2.0 * np.pi * np.outer(c, c) / 256.0
    cs = np.concatenate([np.cos(ang), np.sin(ang)], axis=1) / 16.0
    csn = np.concatenate([np.cos(ang), -np.sin(ang)], axis=1) / 16.0
    n = np.arange(4096, dtype=np.float64)
    k = np.arange(2048 * j, 2048 * (j + 1), dtype=np.float64)
    nk_ = np.mod(np.outer(n, k), 4096.0)
    ang2 = 2.0 * np.pi * nk_ / 4096.0
    dc = np.cos(ang2) / 64.0
    ds = -np.sin(ang2) / 64.0
    tok = np.arange(2048 * j, 2048 * (j + 1))
    row = (tok // 64).astype(np.float32)
    col = (tok % 64).astype(np.float32)
    half = 32
    inv = (1.0 / (np.float32(10000.0) ** (np.arange(half // 2, dtype=np.float32) * np.float32(2.0) / np.float32(half)))).astype(np.float32)
    rc = np.zeros((128, 2048), np.float32)
    rs = np.zeros((128, 2048), np.float32)
    for p in range(128):
        e = p % 64
        pos = row if e < 32 else col
        a = (pos * inv[e % 16]).astype(np.float32)
        rc[p] = np.cos(a)
        rs[p] = np.sin(a)
    blk = np.zeros((128, 128), np.float32)
    blk[:64, :64] = 1.0
    blk[64:, 64:] = 1.0
    prm = np.zeros((128, 128), np.float32)
    for m in range(128):
        if (m % 32) < 16:
            prm[m + 16, m] = -1.0
        else:
            prm[m - 16, m] = 1.0
    return {
        "c_cs": cs.astype(np.float32).astype(bf), "c_csn": csn.astype(np.float32).astype(bf),
        "c_dc": dc.astype(np.float32).astype(bf), "c_ds": ds.astype(np.float32).astype(bf),
        "c_rc": rc, "c_rs": rs,
        "c_idb": np.eye(128, dtype=np.float32).astype(bf), "c_idf": np.eye(128, dtype=np.float32),
        "c_blk": blk.astype(bf), "c_prm": prm.astype(bf),
    }


_NC_CACHE = {}


def _in_maps(inp, nl=DEPTH):
    f = lambda a: np.ascontiguousarray(np.asarray(a, dtype=np.float32))
    xp_, xs_ = f(inp["x_prompt"]), f(inp["x_sample"])
    ck_, cv_ = f(inp["cache_k"]), f(inp["cache_v"])
    lam4 = np.ascontiguousarray(np.stack([f(inp["lam_q1"]), f(inp["lam_k1"]), f(inp["lam_q2"]), f(inp["lam_k2"])], axis=1))
    shared = {k: f(inp[k])[:nl] for k in ("w_in", "w_fproj", "w_aproj", "w_out", "w_mod", "b_mod", "g_norm", "g_q", "g_k", "g_sub")}
    shared["lam4"] = lam4[:nl]
    cst = [_consts(0), _consts(1)]
    maps = []
    for core in range(8):
        b, j = core // 2, core % 2
        m = dict(shared)
        m.update(cst[j])
        m["xp"] = np.ascontiguousarray(xp_[2 * core:2 * core + 2].reshape(512, D))
        m["xs"] = np.ascontiguousarray(xs_[b, 2048 * j:2048 * (j + 1)])
        m["cvec"] = np.ascontiguousarray(np.stack([f(inp["c_ctx"]), f(inp["c"])[b]], axis=0))
        m["ck"] = np.ascontiguousarray(ck_[b].reshape(DEPTH, 512, 1024)[:nl])
        m["cv"] = np.ascontiguousarray(cv_[b].reshape(DEPTH, 512, 1024)[:nl])
        maps.append(m)
    return maps


def kernel(**inputs):
    if "nc" not in _NC_CACHE:
        _NC_CACHE["nc"] = build_program()
    nc = _NC_CACHE["nc"]
    maps = _in_maps(inputs)
    res = run_bass_kernel_spmd(nc, maps, core_ids=list(range(8)))
    r = res.results
    y_p = np.concatenate([r[c]["yp"].reshape(2, 256, D) for c in range(8)], axis=0).astype(np.float32)
    y_s = np.stack([np.concatenate([r[2 * b]["ys"], r[2 * b + 1]["ys"]], axis=0) for b in range(4)], axis=0).astype(np.float32)
    nk_ = np.concatenate([r[c]["nk"] for c in range(8)], axis=0).reshape(16, DEPTH, 256, NH, 2, 64).astype(np.float32)
    nv_ = np.concatenate([r[c]["nv"] for c in range(8)], axis=0).reshape(16, DEPTH, 256, NH, 128).astype(np.float32)
    return (y_p, y_s, nk_, nv_)
```
